# Optimizing a Trainium2 kernel written in Bass

```python
import jax, jax.numpy as jnp
from jax import lax
import numpy as np

D_MODEL = 1024
BATCH = 4
SEQ = 4096
DEPTH = 2

GRID_W = 64
CTX_LEN = 256
HEAD_DIM = 64
N_Q_HEADS = 8
N_KV_HEADS = 2
GQA_GROUP = N_Q_HEADS // N_KV_HEADS
ATTN_WIDTH = N_Q_HEADS * HEAD_DIM
KV_WIDTH = N_KV_HEADS * HEAD_DIM
CONV_GROUPS = 4
CONV_WIDTH = CONV_GROUPS * 64
CONV_K = 3
FOURIER_GROUPS = 4
FOURIER_GROUP_DIM = 64
FOURIER_WIDTH = FOURIER_GROUPS * FOURIER_GROUP_DIM
N_BRANCHES = 3
D_FF = 4 * D_MODEL
Q_BLOCK = 128
ROPE_THETA = 10000.0
ROPE_HALF = HEAD_DIM // 2
EPS = 1e-6
N_MOD = 6

OFF_B = 0
OFF_C = OFF_B + CONV_WIDTH
OFF_X = OFF_C + CONV_WIDTH
OFF_F = OFF_X + CONV_WIDTH
OFF_Q = OFF_F + FOURIER_WIDTH
OFF_K = OFF_Q + ATTN_WIDTH
OFF_V = OFF_K + KV_WIDTH
OFF_G = OFF_V + KV_WIDTH
IN_WIDTH = OFF_G + N_BRANCHES * D_MODEL

kernel_name = "hybrid_conv_fourier_gqa_diffusion_block"


def rms_norm(x, g):
    xf = x.astype(jnp.float32)
    y = xf * lax.rsqrt(jnp.mean(xf * xf, axis=-1, keepdims=True) + EPS)
    return (y * g.astype(jnp.float32)).astype(x.dtype)


def modulate(x, g, shift, scale):
    return rms_norm(x, g) * (1 + scale) + shift


def axial_rope_tables(rows):
    n_freq = ROPE_HALF // 2
    inv = ROPE_THETA ** (-jnp.arange(n_freq, dtype=jnp.float32) / n_freq)
    row_ang = jnp.repeat(jnp.arange(rows, dtype=jnp.float32)[:, None] * inv, GRID_W, axis=0)
    col_ang = jnp.tile(jnp.arange(GRID_W, dtype=jnp.float32)[:, None] * inv, (rows, 1))
    ang = jnp.concatenate([row_ang, col_ang], axis=-1)
    return jnp.cos(ang), jnp.sin(ang)


def apply_rope(x, cos, sin):
    xf = x.astype(jnp.float32)
    x1, x2 = xf[..., :ROPE_HALF], xf[..., ROPE_HALF:]
    c = cos[None, :, None, :]
    s = sin[None, :, None, :]
    return jnp.concatenate([x1 * c - x2 * s, x1 * s + x2 * c], axis=-1).astype(x.dtype)


def short_conv(u, w):
    L = u.shape[1]
    pad = CONV_K // 2
    up = jnp.pad(u, ((0, 0), (pad, pad), (0, 0)))
    return sum(up[:, j:j + L] * w[j] for j in range(CONV_K))


def fourier_mix(u):
    b, L, _ = u.shape
    ug = u.astype(jnp.float32).reshape(b, L, FOURIER_GROUPS, FOURIER_GROUP_DIM)
    y = jnp.fft.fft2(ug, axes=(1, 3), norm="ortho").real
    return y.reshape(b, L, FOURIER_WIDTH).astype(u.dtype)


def local_branches(z, w_conv):
    y_conv = z[..., OFF_B:OFF_C] * short_conv(z[..., OFF_C:OFF_X] * z[..., OFF_X:OFF_F], w_conv)
    y_four = fourier_mix(z[..., OFF_F:OFF_Q])
    return y_conv, y_four


def qkv_heads(z, g_q, g_k):
    b, L, _ = z.shape
    q = rms_norm(z[..., OFF_Q:OFF_K].reshape(b, L, N_Q_HEADS, HEAD_DIM), g_q)
    k = rms_norm(z[..., OFF_K:OFF_V].reshape(b, L, N_KV_HEADS, HEAD_DIM), g_k)
    v = z[..., OFF_V:OFF_G].reshape(b, L, N_KV_HEADS, HEAD_DIM)
    return q, k, v


def gqa(q, k, v):
    b, lq = q.shape[:2]
    qg = q.reshape(b, lq, N_KV_HEADS, GQA_GROUP, HEAD_DIM)
    s = jnp.einsum('bqgrd,bkgd->bgrqk', qg, k).astype(jnp.float32) * (HEAD_DIM ** -0.5)
    p = jax.nn.softmax(s, axis=-1).astype(v.dtype)
    o = jnp.einsum('bgrqk,bkgd->bqgrd', p, v)
    return o.reshape(b, lq, ATTN_WIDTH)


def blocked_gqa(q, k, v):
    b, lq = q.shape[:2]
    nb = lq // Q_BLOCK
    qb = q.reshape(b, nb, Q_BLOCK, N_Q_HEADS, HEAD_DIM).transpose(1, 0, 2, 3, 4)
    ob = lax.map(lambda qi: gqa(qi, k, v), qb)
    return ob.transpose(1, 0, 2, 3).reshape(b, lq, ATTN_WIDTH)


def merge_branches(z, y_conv, y_four, y_attn, w_conv_out, w_four_out, w_attn_out, w_o):
    gates = jax.nn.sigmoid(z[..., OFF_G:].astype(jnp.float32)).astype(z.dtype)
    g_conv, g_four, g_attn = jnp.split(gates, N_BRANCHES, axis=-1)
    m = g_conv * (y_conv @ w_conv_out) + g_four * (y_four @ w_four_out) + g_attn * (y_attn @ w_attn_out)
    return m @ w_o


def sq_relu_mlp(h, w1, w2):
    return jnp.square(jax.nn.relu(h @ w1)) @ w2


def setup_inputs(seed: int = 0) -> dict:
    key = jax.random.key(seed)
    ks = jax.random.split(key, 20)
    f32 = jnp.float32
    nrm = lambda k, shape, s: jax.random.normal(k, shape, f32) * s
    return {
        "x": nrm(ks[0], (BATCH, SEQ, D_MODEL), 1.0),
        "c": nrm(ks[1], (BATCH, D_MODEL), 1.0),
        "ctx": nrm(ks[2], (BATCH, CTX_LEN, D_MODEL), 1.0),
        "c_ctx": nrm(ks[3], (D_MODEL,), 1.0),
        "w_mod": nrm(ks[4], (DEPTH, D_MODEL, N_MOD * D_MODEL), 0.5 * D_MODEL ** -0.5),
        "b_mod": nrm(ks[5], (DEPTH, N_MOD * D_MODEL), 0.01),
        "g_norm1": 1.0 + nrm(ks[6], (DEPTH, D_MODEL), 0.05),
        "g_norm2": 1.0 + nrm(ks[7], (DEPTH, D_MODEL), 0.05),
        "w_in": nrm(ks[8], (DEPTH, D_MODEL, IN_WIDTH), D_MODEL ** -0.5),
        "w_conv": nrm(ks[9], (DEPTH, CONV_K, CONV_WIDTH), CONV_K ** -0.5),
        "g_q": 1.0 + nrm(ks[10], (DEPTH, HEAD_DIM), 0.05),
        "g_k": 1.0 + nrm(ks[11], (DEPTH, HEAD_DIM), 0.05),
        "w_conv_out": nrm(ks[12], (DEPTH, CONV_WIDTH, D_MODEL), CONV_WIDTH ** -0.5),
        "w_four_out": nrm(ks[13], (DEPTH, FOURIER_WIDTH, D_MODEL), FOURIER_WIDTH ** -0.5),
        "w_attn_out": nrm(ks[14], (DEPTH, ATTN_WIDTH, D_MODEL), ATTN_WIDTH ** -0.5),
        "w_o": nrm(ks[15], (DEPTH, D_MODEL, D_MODEL), D_MODEL ** -0.5),
        "w_ff1": nrm(ks[16], (DEPTH, D_MODEL, D_FF), D_MODEL ** -0.5),
        "w_ff2": nrm(ks[17], (DEPTH, D_FF, D_MODEL), D_FF ** -0.5),
    }


def reference(x, c, ctx, c_ctx, w_mod, b_mod, g_norm1, g_norm2, w_in, w_conv, g_q, g_k,
              w_conv_out, w_four_out, w_attn_out, w_o, w_ff1, w_ff2):
    S = x.shape[1]
    ROWS = S // GRID_W
    cos, sin = axial_rope_tables(ROWS)
    ctx_s = ctx
    for l in range(DEPTH):
        mod = (jax.nn.silu(c) @ w_mod[l] + b_mod[l])[:, None, :]
        mod_c = (jax.nn.silu(c_ctx)[None] @ w_mod[l] + b_mod[l])[:, None, :]
        sh1, sc1, gt1, sh2, sc2, gt2 = jnp.split(mod, N_MOD, axis=-1)
        csh1, csc1, cgt1, csh2, csc2, cgt2 = jnp.split(mod_c, N_MOD, axis=-1)

        hc = modulate(ctx_s, g_norm1[l], csh1, csc1)
        zc = hc @ w_in[l]
        qc, kc, vc = qkv_heads(zc, g_q[l], g_k[l])

        h = modulate(x, g_norm1[l], sh1, sc1)
        z = h @ w_in[l]
        q, k, v = qkv_heads(z, g_q[l], g_k[l])
        q = apply_rope(q, cos, sin)
        k = apply_rope(k, cos, sin)
        k_all = jnp.concatenate([k, kc], axis=1)
        v_all = jnp.concatenate([v, vc], axis=1)
        y_attn = blocked_gqa(q, k_all, v_all)
        y_conv, y_four = local_branches(z, w_conv[l])
        x = x + gt1 * merge_branches(z, y_conv, y_four, y_attn,
                                     w_conv_out[l], w_four_out[l], w_attn_out[l], w_o[l])
        x = x + gt2 * sq_relu_mlp(modulate(x, g_norm2[l], sh2, sc2), w_ff1[l], w_ff2[l])

        if l < DEPTH - 1:
            yc_attn = gqa(qc, kc, vc)
            yc_conv, yc_four = local_branches(zc, w_conv[l])
            ctx_s = ctx_s + cgt1 * merge_branches(zc, yc_conv, yc_four, yc_attn,
                                                  w_conv_out[l], w_four_out[l], w_attn_out[l], w_o[l])
            ctx_s = ctx_s + cgt2 * sq_relu_mlp(modulate(ctx_s, g_norm2[l], csh2, csc2),
                                               w_ff1[l], w_ff2[l])
    return x
```

```python
import types
import numpy as np
import ml_dtypes
from contextlib import ExitStack
import concourse.bass as bass
import concourse.mybir as mybir
from concourse.bass_utils import run_bass_kernel_spmd

F32 = mybir.dt.float32
BF16 = mybir.dt.bfloat16
AF = mybir.ActivationFunctionType
ALU = mybir.AluOpType
AX = mybir.AxisListType
ENGS = ("pe", "act", "dve", "pool", "sp")

DEPTH = 2
D = 1024
NT = 2304
CH = [(0, 512), (512, 512), (1024, 512), (1536, 512), (2048, 256)]
OFF_B, OFF_C, OFF_X, OFF_F, OFF_Q, OFF_K, OFF_V, OFF_G = 0, 256, 512, 768, 1024, 1536, 1664, 1792
EPS = 1e-6


def _freeze(fn):
    if fn is None or fn.__closure__ is None:
        return fn
    cells = []
    for c in fn.__closure__:
        try:
            cells.append(types.CellType(c.cell_contents))
        except ValueError:
            cells.append(c)
    cells = tuple(cells)
    return types.FunctionType(fn.__code__, fn.__globals__, fn.__name__, fn.__defaults__, cells)


class Prog:
    def __init__(self, nc, es):
        self.nc = nc
        self.es = es
        self.ops = {e: [] for e in ENGS}
        self.kw = {}
        self.kr = {}
        self.dsems = {}
        self.esem = {e: es.enter_context(nc.semaphore("es_" + e)) for e in ENGS}
        self.waited = {e: {} for e in ENGS}

    def dsem(self, name):
        if name not in self.dsems:
            h = self.es.enter_context(self.nc.semaphore("ds_" + name))
            self.dsems[name] = [h, 0]
        return self.dsems[name]

    def _filter(self, eng, deps, is_dma):
        out = []
        for ev in deps:
            if ev[0] == "e" and ev[1] == eng and eng == "pe":
                continue
            skey = (ev[0], ev[1])
            if ev[2] <= self.waited[eng].get(skey, -1):
                continue
            self.waited[eng][skey] = ev[2]
            out.append(ev)
        return out

    def _deps(self, eng, reads, writes, is_dma, dma_sem=None):
        deps = []
        for k in reads:
            ev = self.kw.get(k)
            if ev is not None:
                deps.append(ev)
        for k in writes:
            ev = self.kw.get(k)
            rl = self.kr.get(k, [])
            if ev is not None:
                if not (is_dma and ev[0] == "d" and ev[1] == dma_sem and not rl):
                    deps.append(ev)
            deps.extend(rl)
        return self._filter(eng, deps, is_dma)

    def _commit(self, ev, reads, writes):
        for k in reads:
            self.kr.setdefault(k, []).append(ev)
        for k in writes:
            self.kw[k] = ev
            self.kr[k] = []

    def op(self, eng, fn, reads=(), writes=()):
        waits = self._deps(eng, reads, writes, False)
        ev = ("e", eng, len(self.ops[eng]))
        self.ops[eng].append(dict(fn=_freeze(fn), waits=waits, ev=ev, dma=False))
        self._commit(ev, reads, writes)

    def dma(self, eng, sem, out, in_, reads=(), writes=(), **kw):
        s = self.dsem(sem)
        waits = self._deps(eng, reads, writes, True, sem)
        s[1] += 16
        ev = ("d", sem, s[1])
        fn = lambda e, out=out, in_=in_, kw=kw: e.dma_start(out=out, in_=in_, **kw)
        self.ops[eng].append(dict(fn=fn, waits=waits, ev=ev, dma=True, inc=16))
        self._commit(ev, reads, writes)

    def custom(self, eng, sem, inc, fn, reads=(), writes=()):
        s = self.dsem(sem)
        waits = self._deps(eng, reads, writes, True, sem)
        s[1] += inc
        ev = ("d", sem, s[1])
        self.ops[eng].append(dict(fn=_freeze(fn), waits=waits, ev=ev, dma=True, inc=inc))
        self._commit(ev, reads, writes)

    def barrier(self):
        evs = []
        for e in ENGS:
            last = None
            for i in range(len(self.ops[e]) - 1, -1, -1):
                o = self.ops[e][i]
                if o["fn"] is not None and not o["dma"]:
                    last = o["ev"]
                    break
            if last is not None:
                evs.append(last)
        for name, (h, cnt) in self.dsems.items():
            if cnt > 0:
                evs.append(("d", name, cnt))
        for e in ENGS:
            waits = self._filter(e, [ev for ev in evs if not (ev[0] == "e" and ev[1] == e)], False)
            self.ops[e].append(dict(fn=None, waits=waits, ev=None, dma=False))
        self.kw = {}
        self.kr = {}

    def emit(self):
        nc = self.nc
        sig = {e: set() for e in ENGS}
        for e in ENGS:
            for o in self.ops[e]:
                for ev in o["waits"]:
                    if ev[0] == "e":
                        sig[ev[1]].add(ev[2])
        val = {}
        for e in ENGS:
            for r, idx in enumerate(sorted(sig[e])):
                val[(e, idx)] = r + 1
        prog = self

        def replay(ename):
            def body(eng):
                for o in prog.ops[ename]:
                    for ev in o["waits"]:
                        if ev[0] == "e":
                            eng.wait_ge(prog.esem[ev[1]], val[(ev[1], ev[2])])
                        else:
                            eng.wait_ge(prog.dsems[ev[1]][0], ev[2])
                    if o["fn"] is None:
                        continue
                    inst = o["fn"](eng)
                    ev = o["ev"]
                    if o["dma"]:
                        inst.then_inc(prog.dsems[ev[1]][0], o["inc"])
                    elif ev[2] in sig[ename]:
                        inst.then_inc(prog.esem[ename], 1)
            return body

        with nc.Block() as block:
            block.tensor(replay("pe"))
            block.scalar(replay("act"))
            block.vector(replay("dve"))
            block.gpsimd(replay("pool"))
            block.sync(replay("sp"))


def build(debug=False):
    nc = bass.Bass("TRN2", target_bir_lowering=False)
    dt_in = lambda name, shape, dt=F32: nc.dram_tensor(name, list(shape), dt, kind="ExternalInput").ap()
    x_in = dt_in("x", [2048, D])
    ctx_in = dt_in("ctx", [256, D])
    cc_in = dt_in("cc", [128, 8, 2])
    wmod_in = dt_in("w_mod", [DEPTH, D, 6 * D])
    bm_in = dt_in("bm", [DEPTH, 128, 48])
    gn1_in = dt_in("gn1", [DEPTH, 128, 8])
    gn2_in = dt_in("gn2", [DEPTH, 128, 8])
    win_in = dt_in("w_in", [DEPTH, D, 4864])
    wcv_in = dt_in("wcv", [DEPTH, 128, 2, 3])
    gqk_in = dt_in("gqk", [DEPTH, 128, 640])
    wco_in = dt_in("w_conv_out", [DEPTH, 256, D])
    wfo_in = dt_in("w_four_out", [DEPTH, 256, D])
    wao_in = dt_in("w_attn_out", [DEPTH, 512, D])
    wo_in = dt_in("w_o", [DEPTH, D, D])
    w1_in = dt_in("w_ff1", [DEPTH, D, 4 * D])
    w2_in = dt_in("w_ff2", [DEPTH, 4 * D, D])
    rcos_in = dt_in("rcos", [128, 16, 32])
    rsin_in = dt_in("rsin", [128, 16, 32])
    dcos_in = dt_in("dcos", [4096, 2048], BF16)
    dsin_in = dt_in("dsin", [4096, 2048], BF16)
    dcc_in = dt_in("dcc", [256, 256], BF16)
    dcs_in = dt_in("dcs", [256, 256], BF16)
    ccb_in = dt_in("ccb", [128, 128], BF16)
    scb_in = dt_in("scb", [128, 128], BF16)
    sel_in = dt_in("sel", [128, 2])
    ident_in = dt_in("ident", [128, 128])
    y_out = nc.dram_tensor("y", [2048, D], F32, kind="ExternalOutput").ap()
    dbg_out = None
    if debug:
        dbg_out = nc.dram_tensor("dbg", [DEPTH, 128, 8, NT], F32, kind="ExternalOutput").ap()
        dd = {}
        for nm, shp, dt_ in (("d_h", [128, 8, NT], BF16), ("d_ycv", [128, 2, NT], BF16), ("d_yfr", [128, 2, NT], BF16),
                             ("d_yatt", [128, 4, NT], BF16), ("d_xmid", [128, 8, NT], F32), ("d_mod", [128, 48, 2], F32),
                             ("d_qt", [128, 4, NT], BF16), ("d_xout", [4096, 512], BF16), ("d_hg", [128, 2, 8], F32)):
            dd[nm] = nc.dram_tensor(nm, shp, dt_, kind="ExternalOutput").ap()

    XD = nc.dram_tensor("XD", [128, 8, NT], F32).ap()
    XIN = [nc.dram_tensor("XIN%d" % l, [2048, 512], BF16) for l in range(DEPTH)]
    XOUT = [nc.dram_tensor("XOUT%d" % l, [4096, 512], BF16) for l in range(DEPTH)]
    HIN = [nc.dram_tensor("HIN%d" % l, [128, 8], F32) for l in range(DEPTH)]
    HOUT = [nc.dram_tensor("HOUT%d" % l, [256, 8], F32) for l in range(DEPTH)]
    RG = [[0, 1], [2, 3], [4, 5], [6, 7]]

    with ExitStack() as es:
        P = Prog(nc, es)
        cnt = [0]

        def sb(st, name, shape, dt=F32):
            cnt[0] += 1
            return st.enter_context(nc.sbuf_tensor("%s_%d" % (name, cnt[0]), list(shape), dt))
        PA = es.enter_context(nc.psum_tensor("PA", [128, 1024], F32))
        PB = es.enter_context(nc.psum_tensor("PB", [128, 1024], F32))
        PC = es.enter_context(nc.psum_tensor("PC", [128, 1024], F32))
        PS1 = es.enter_context(nc.psum_tensor("PS1", [128, 512], F32))
        PT = es.enter_context(nc.psum_tensor("PT", [128, 1024], BF16))

        def bank(t, name, h):
            return (t[:, h * 512:(h + 1) * 512], (name, h))
        A0, A1 = bank(PA, "PA", 0), bank(PA, "PA", 1)
        B0, B1 = bank(PB, "PB", 0), bank(PB, "PB", 1)
        C0, C1 = bank(PC, "PC", 0), bank(PC, "PC", 1)
        S1 = (PS1[:, :], ("PS1", 0))

        identF = sb(es, "identF", [128, 128])
        identB = sb(es, "identB", [128, 128], BF16)
        onesB = sb(es, "onesB", [128, 128], BF16)
        onesF = sb(es, "onesF", [128, 64])
        RC = sb(es, "RC", [128, 16, 32])
        RS = sb(es, "RS", [128, 16, 32])
        SEL = sb(es, "SEL", [128, 2])
        CCB = sb(es, "CCB", [128, 128], BF16)
        SCB = sb(es, "SCB", [128, 128], BF16)
        DCC = sb(es, "DCC", [128, 2, 256], BF16)
        DCS = sb(es, "DCS", [128, 2, 256], BF16)
        CCF = sb(es, "CCF", [128, 8, 2])
        SCC = sb(es, "SCC", [128, 8, 2], BF16)
        MOD = sb(es, "MOD", [128, 48, 2])
        G1 = sb(es, "G1", [128, 8, 2])
        G2 = sb(es, "G2", [128, 8, 2])
        GN1 = sb(es, "GN1", [128, 8])
        GN2 = sb(es, "GN2", [128, 8])
        BM = sb(es, "BM", [128, 48])
        WCV = sb(es, "WCV", [128, 2, 3])
        GQK = sb(es, "GQK", [128, 640])
        CKV = sb(es, "CKV", [128, 2, 512], BF16)
        HS = sb(es, "HS", [128, 8])
        HG = sb(es, "HG", [128, 2, 8])

        P.dma("sp", "c0", identF[:], ident_in, writes=["identF"])
        P.dma("sp", "c1", RC[:], rcos_in, writes=["RC"])
        P.dma("sp", "c2", RS[:], rsin_in, writes=["RS"])
        P.dma("sp", "c3", SEL[:], sel_in, writes=["SEL"])
        P.dma("sp", "c4", CCB[:], ccb_in, writes=["CCB"])
        P.dma("sp", "c5", SCB[:], scb_in, writes=["SCB"])
        P.dma("sp", "c6", DCC[:], dcc_in.rearrange("(t p) k -> p t k", p=128), writes=["DCC"])
        P.dma("sp", "c7", DCS[:], dcs_in.rearrange("(t p) k -> p t k", p=128), writes=["DCS"])
        P.dma("sp", "c8", CCF[:], cc_in, writes=["CCF"])
        P.op("dve", lambda e: e.tensor_copy(out=identB[:], in_=identF[:]), ["identF"], ["identB"])
        P.op("pool", lambda e: e.memset(onesB[:], 1.0), [], ["onesB"])
        P.op("pool", lambda e: e.memset(onesF[:], 1.0), [], ["onesF"])
        P.op("pool", lambda e: e.memset(HS[:], 0.0), [], ["HS"])
        P.op("act", lambda e: e.activation(out=SCC[:], in_=CCF[:], func=AF.Silu), ["CCF"], ["SCC"])

        with ExitStack() as st:
            XS = sb(st, "XS", [128, 4, D])
            XO = sb(st, "XO", [128, 8, 512])
            for ci, (c0, N) in enumerate(CH):
                nt = N // 128
                if ci < 4:
                    src = x_in[c0:c0 + N, :]
                else:
                    src = ctx_in
                P.dma("sp", "XS", XS[:, 0:nt, :], src.rearrange("(t p) f -> p t f", p=128),
                      writes=["XS"])
                for j in range(8):
                    bk = [A0, A1, B0, B1][j % 4]
                    for ti in range(nt):
                        P.op("pe", lambda e, bk=bk, ti=ti, j=j: e.transpose(
                            bk[0][:, ti * 128:(ti + 1) * 128], XS[:, ti, j * 128:(j + 1) * 128], identF[:]),
                            ["XS", "identF"], [bk[1]])
                    eng = "dve" if j % 2 == 0 else "act"
                    if eng == "dve":
                        P.op("dve", lambda e, bk=bk, j=j, N=N: e.tensor_copy(out=XO[:, j, 0:N], in_=bk[0][:, 0:N]),
                             [bk[1]], [("XO", j)])
                    else:
                        P.op("act", lambda e, bk=bk, j=j, N=N: e.activation(out=XO[:, j, 0:N], in_=bk[0][:, 0:N], func=AF.Copy),
                             [bk[1]], [("XO", j)])
                P.dma("sp", "XO", XD[:, :, c0:c0 + N], XO[:, :, 0:N],
                      reads=[("XO", j) for j in range(8)], writes=[("XD", ci)])
        P.barrier()

        for l in range(DEPTH):
            last = (l == DEPTH - 1)
            P.dma("sp", "p0", BM[:], bm_in[l], writes=["BM"])
            P.dma("sp", "p1", GN1[:], gn1_in[l], writes=["GN1"])
            P.dma("sp", "p2", GN2[:], gn2_in[l], writes=["GN2"])
            P.dma("sp", "p3", WCV[:], wcv_in[l], writes=["WCV"])
            P.dma("sp", "p4", GQK[:], gqk_in[l], writes=["GQK"])
            with ExitStack() as st:
                WM = sb(st, "WM", [128, 2, 8, 512], BF16)
                for pc in range(12):
                    bf = pc % 2
                    P.dma("pool", "WM%d" % bf, WM[:, bf],
                          wmod_in[l][:, pc * 512:(pc + 1) * 512].rearrange("(k p) n -> p k n", p=128),
                          writes=[("WM", bf)])
                    for ct in range(4):
                        col = (pc * 4 + ct) * 2
                        for kc in range(8):
                            P.op("pe", lambda e, bf=bf, ct=ct, kc=kc, col=col: e.matmul(
                                S1[0][:, col:col + 2], WM[:, bf, kc, ct * 128:(ct + 1) * 128], SCC[:, kc, :],
                                start=(kc == 0), stop=(kc == 7)),
                                [("WM", bf), "SCC"], [S1[1]])
                P.op("dve", lambda e: e.tensor_tensor(
                    out=MOD[:], in0=S1[0][:, 0:96].rearrange("p (t s) -> p t s", s=2),
                    in1=BM[:, :].unsqueeze(2).to_broadcast([128, 48, 2]), op=ALU.add),
                    [S1[1], "BM"], ["MOD"])
                for (Gt, GN, o, nm) in ((G1, GN1, 8, "G1"), (G2, GN2, 32, "G2")):
                    P.op("dve", lambda e, Gt=Gt, o=o: e.tensor_scalar(
                        out=Gt[:], in0=MOD[:, o:o + 8, :], scalar1=1.0, scalar2=None, op0=ALU.add),
                        ["MOD"], [nm])
                    P.op("dve", lambda e, Gt=Gt, GN=GN: e.tensor_tensor(
                        out=Gt[:], in0=Gt[:], in1=GN[:, :].unsqueeze(2).to_broadcast([128, 8, 2]), op=ALU.mult),
                        [nm, "GN1", "GN2"], [nm])
            P.barrier()

            def normalize(st_bufs, ci, Gt, Gname, sh_off, HTt):
                XC, SQ, RSTD = st_bufs
                TMP = XC
                TMP2 = XC
                c0, N = CH[ci]
                s = 1 if ci == 4 else 0
                P.dma("sp", "XC", XC[:, :, 0:N], XD[:, :, c0:c0 + N], reads=[("XD", ci)], writes=["XC"])
                P.op("act", lambda e: e.activation(out=SQ[:, :, 0:N], in_=XC[:, :, 0:N], func=AF.Square),
                     ["XC"], ["SQ"])
                for j in range(8):
                    P.op("pe", lambda e, j=j: e.matmul(S1[0][:, 0:N], onesB[:], SQ[:, j, 0:N],
                                                      start=(j == 0), stop=(j == 7)),
                         ["SQ", "onesB"], [S1[1]])
                P.op("act", lambda e: e.activation(out=RSTD[:, 0:N], in_=S1[0][:, 0:N], func=AF.Sqrt,
                                                   scale=1.0 / D, bias=EPS),
                     [S1[1]], ["RSTD"])
                P.op("dve", lambda e: e.reciprocal(out=RSTD[:, 0:N], in_=RSTD[:, 0:N]),
                     ["RSTD"], ["RSTD"])
                P.op("dve", lambda e: e.tensor_tensor(
                    out=TMP[:, :, 0:N], in0=XC[:, :, 0:N],
                    in1=RSTD[:, 0:N].unsqueeze(1).to_broadcast([128, 8, N]), op=ALU.mult),
                    ["XC", "RSTD"], ["XC"])
                P.op("pool", lambda e: e.tensor_tensor(
                    out=TMP2[:, :, 0:N], in0=TMP[:, :, 0:N],
                    in1=Gt[:, :, s:s + 1].to_broadcast([128, 8, N]), op=ALU.mult),
                    ["XC", Gname], ["XC"])
                P.op("dve", lambda e: e.tensor_tensor(
                    out=HTt[:, :, c0:c0 + N], in0=TMP2[:, :, 0:N],
                    in1=MOD[:, sh_off:sh_off + 8, s:s + 1].to_broadcast([128, 8, N]), op=ALU.add),
                    ["XC", "MOD"], [("HT", ci)])

            with ExitStack() as slay:
                HT = sb(slay, "HT", [128, 8, NT], BF16)
                YATT = sb(slay, "YATT", [128, 4, NT], BF16)
                YCV = sb(slay, "YCV", [128, 2, NT], BF16)
                YFR = sb(slay, "YFR", [128, 2, NT], BF16)
                with ExitStack() as sq_:
                    QT = sb(sq_, "QT", [128, 4, NT], BF16)
                    U = sb(sq_, "U", [128, 2, 2308], BF16)
                    ZB = sb(sq_, "ZB", [128, 2, NT], BF16)
                    P.op("pool", lambda e: e.memset(U[:], 0.0), [], ["U"])
                    with ExitStack() as st:
                        WB = sb(st, "WB", [128, 8, 1792], BF16)
                        XC = sb(st, "XC", [128, 8, 512])
                        SQ = sb(st, "SQ", [128, 8, 512], BF16)
                        RSTD = sb(st, "RSTD", [128, 512])
                        CX = sb(st, "CX", [128, 512])
                        SQ2 = sb(st, "SQ2", [128, 640])
                        SS = sb(st, "SS", [128, 10])
                        RQ = sb(st, "RQ", [128, 10])
                        QN = sb(st, "QN", [128, 640])
                        QG = sb(st, "QG", [128, 640])
                        T1 = sb(st, "T1", [128, 320])
                        T2 = sb(st, "T2", [128, 320])
                        T3 = sb(st, "T3", [128, 320])
                        T4 = sb(st, "T4", [128, 320])
                        QR = sb(st, "QR", [128, 640], BF16)
                        XST = sb(st, "XST", [128, 2, 512], BF16)
                        wsrc = win_in[l]
                        def wload(dst0, n, src0, key):
                            P.dma("pool", "WB", WB[:, :, dst0:dst0 + n],
                                  wsrc[:, src0:src0 + n].rearrange("(k p) n -> p k n", p=128), writes=[key])
                        wload(0, 768, OFF_B, "WB")
                        wload(768, 128, OFF_V, "WB")
                        wload(896, 256, OFF_F, "WB")
                        wload(1152, 128, OFF_K, "WB")
                        for g in range(2):
                            for jj in range(4):
                                wload(1280 + jj * 128 + g * 64, 64, OFF_Q + (g * 4 + jj) * 64, "WB")
                        nbufs = (XC, SQ, RSTD)
                        for ci, (c0, N) in enumerate(CH):
                            normalize(nbufs, ci, G1, "G1", 0, HT)
                            for t in range(2):
                                for (bk, off) in ((A0, OFF_B), (A1, OFF_C), (B0, OFF_X)):
                                    for kc in range(8):
                                        P.op("pe", lambda e, bk=bk, off=off, t=t, kc=kc, c0=c0, N=N: e.matmul(
                                            bk[0][:, 0:N], WB[:, kc, off + t * 128: off + (t + 1) * 128],
                                            HT[:, kc, c0:c0 + N], start=(kc == 0), stop=(kc == 7)),
                                            ["WB", ("HT", ci)], [bk[1]])
                                P.op("act", lambda e, N=N: e.activation(out=CX[:, 0:N], in_=B0[0][:, 0:N], func=AF.Copy),
                                     [B0[1]], ["CX"])
                                ub = (1 + c0) if ci < 4 else 2051
                                P.op("dve", lambda e, t=t, ub=ub, N=N: e.tensor_tensor(
                                    out=U[:, t, ub:ub + N], in0=A1[0][:, 0:N], in1=CX[:, 0:N], op=ALU.mult),
                                    [A1[1], "CX"], [("U", ci)])
                                P.op("act", lambda e, t=t, c0=c0, N=N: e.activation(
                                    out=ZB[:, t, c0:c0 + N], in_=A0[0][:, 0:N], func=AF.Copy),
                                    [A0[1]], [("ZB", ci, t)])
                            for ti in range(N // 128):
                                tg = c0 // 128 + ti
                                t0 = c0 + ti * 128
                                for half in range(2):
                                    for kc in range(8):
                                        P.op("pe", lambda e, half=half, kc=kc, t0=t0: e.matmul(
                                            PC[:, half * 512:(half + 1) * 512], HT[:, kc, t0:t0 + 128],
                                            WB[:, kc, 768 + half * 512: 768 + (half + 1) * 512],
                                            start=(kc == 0), stop=(kc == 7)),
                                            ["WB", ("HT", ci)], [("PC", half)])
                                PCk = [("PC", 0), ("PC", 1)]
                                P.op("act", lambda e: e.activation(out=SQ2[:], in_=PC[:, 384:1024], func=AF.Square),
                                     PCk, ["SQ2"])
                                P.op("dve", lambda e: e.tensor_reduce(
                                    out=SS[:], in_=SQ2[:, :].rearrange("p (h d) -> p h d", d=64), axis=AX.X, op=ALU.add),
                                    ["SQ2"], ["SS"])
                                P.op("act", lambda e: e.activation(out=RQ[:], in_=SS[:], func=AF.Sqrt,
                                                                   scale=1.0 / 64, bias=EPS),
                                     ["SS"], ["RQ"])
                                P.op("dve", lambda e: e.reciprocal(out=RQ[:], in_=RQ[:]), ["RQ"], ["RQ"])
                                P.op("dve", lambda e: e.tensor_tensor(
                                    out=QN[:, :].rearrange("p (h d) -> p h d", d=64),
                                    in0=PC[:, 384:1024].rearrange("p (h d) -> p h d", d=64),
                                    in1=RQ[:, :].unsqueeze(2).to_broadcast([128, 10, 64]), op=ALU.mult),
                                    PCk + ["RQ"], ["QN"])
                                P.op("pool", lambda e: e.tensor_tensor(out=QG[:], in0=QN[:], in1=GQK[:], op=ALU.mult),
                                     ["QN", "GQK"], ["QG"])
                                QG3 = QG[:, :].rearrange("p (h d) -> p h d", d=64)
                                QR3 = QR[:, :].rearrange("p (h d) -> p h d", d=64)
                                if ci < 4:
                                    cb = RC[:, tg, :].unsqueeze(1).to_broadcast([128, 10, 32])
                                    sbb = RS[:, tg, :].unsqueeze(1).to_broadcast([128, 10, 32])
                                    v3 = lambda T: T[:, :].rearrange("p (h d) -> p h d", d=32)
                                    P.op("dve", lambda e, cb=cb: e.tensor_tensor(out=v3(T1), in0=QG3[:, :, 0:32], in1=cb, op=ALU.mult),
                                         ["QG", "RC"], ["T1"])
                                    P.op("pool", lambda e, sbb=sbb: e.tensor_tensor(out=v3(T2), in0=QG3[:, :, 32:64], in1=sbb, op=ALU.mult),
                                         ["QG", "RS"], ["T2"])
                                    P.op("pool", lambda e, sbb=sbb: e.tensor_tensor(out=v3(T3), in0=QG3[:, :, 0:32], in1=sbb, op=ALU.mult),
                                         ["QG", "RS"], ["T3"])
                                    P.op("dve", lambda e, cb=cb: e.tensor_tensor(out=v3(T4), in0=QG3[:, :, 32:64], in1=cb, op=ALU.mult),
                                         ["QG", "RC"], ["T4"])
                                    P.op("dve", lambda e: e.tensor_tensor(out=QR3[:, :, 0:32], in0=v3(T1), in1=v3(T2), op=ALU.subtract),
                                         ["T1", "T2"], ["QRa"])
                                    P.op("pool", lambda e: e.tensor_tensor(out=QR3[:, :, 32:64], in0=v3(T3), in1=v3(T4), op=ALU.add),
                                         ["T3", "T4"], ["QRb"])
                                else:
                                    P.op("pool", lambda e: e.tensor_copy(out=QR[:], in_=QG[:]), ["QG"], ["QRa", "QRb"])
                                if ci < 4:
                                    xb = tg % 2
                                    dst = XST[:, xb, :]
                                    dkey = ("XST", xb)
                                else:
                                    dst = CKV[:, ti, :]
                                    dkey = ("CKV", ti)
                                P.op("act", lambda e, dst=dst: e.activation(out=dst[:, 0:384], in_=PC[:, 0:384], func=AF.Copy),
                                     [("PC", 0)], [dkey])
                                P.op("pool", lambda e, dst=dst: e.tensor_copy(out=dst[:, 384:512], in_=QR[:, 0:128]),
                                     ["QRa", "QRb"], [(dkey, "k")])
                                if ci < 4:
                                    P.dma("sp", "XST%d" % xb, XIN[l].ap()[t0:t0 + 128, :], dst,
                                          reads=[dkey, (dkey, "k")], writes=["XIN"])
                                for i in range(4):
                                    P.op("pe", lambda e, i=i: e.transpose(
                                        PT[:, i * 128:(i + 1) * 128], QR[:, 128 + i * 128: 256 + i * 128], identB[:]),
                                        ["QRa", "QRb", "identB"], ["PT"])
                                P.op("dve", lambda e, t0=t0: e.tensor_copy(
                                    out=QT[:, :, t0:t0 + 128], in_=PT[:, 0:512].rearrange("p (j t) -> p j t", t=128)),
                                    ["PT"], [("QT", tg)])
                        for t in range(2):
                            P.op("dve", lambda e, t=t: e.tensor_copy(out=HS[:, 2 * t:2 * t + 1], in_=U[:, t, 1:2]), [("U", 0)], ["HS"])
                            P.op("dve", lambda e, t=t: e.tensor_copy(out=HS[:, 2 * t + 1:2 * t + 2], in_=U[:, t, 2048:2049]), [("U", 3)], ["HS"])
                        P.dma("sp", "HS", HIN[l].ap(), HS[:], reads=["HS"], writes=["HIN"])
                    P.barrier()
                    if debug and l == 0:
                        P.dma("sp", "dbg", dd["d_h"], HT[:], writes=["dbg"])
                        P.dma("sp", "dbg", dd["d_qt"], QT[:], writes=["dbg"])
                        P.dma("sp", "dbg", dd["d_mod"], MOD[:], writes=["dbg"])
                    P.custom("pool", "cc", 1, lambda e, l=l: e.collective_compute(
                        "AllGather", ALU.bypass, replica_groups=RG,
                        ins=[XIN[l].ap().opt()], outs=[XOUT[l].ap().opt()]),
                        reads=["XIN"], writes=["XOUT"])
                    P.custom("pool", "cc", 1, lambda e, l=l: e.collective_compute(
                        "AllGather", ALU.bypass, replica_groups=RG,
                        ins=[HIN[l].ap().opt()], outs=[HOUT[l].ap().opt()]),
                        reads=["HIN"], writes=["HOUT"])
                    P.barrier()
                    with ExitStack() as st:
                        ACC = sb(st, "ACC", [128, 2, 512])
                        P.dma("sp", "HG", HG[:], HOUT[l].ap().rearrange("(r p) f -> p r f", p=128),
                              reads=["HOUT"], writes=["HG"])
                        for t in range(2):
                            P.op("dve", lambda e, t=t: e.tensor_tensor(out=U[:, t, 0:1], in0=HG[:, 0, 2 * t + 1:2 * t + 2],
                                                                      in1=SEL[:, 0:1], op=ALU.mult), ["HG", "SEL", "U"], ["U"])
                            P.op("dve", lambda e, t=t: e.tensor_tensor(out=U[:, t, 2049:2050], in0=HG[:, 1, 2 * t:2 * t + 1],
                                                                      in1=SEL[:, 1:2], op=ALU.mult), ["HG", "SEL", "U"], ["U"])
                        for ci, (c0, N) in enumerate(CH):
                            b = c0 if ci < 4 else 2050
                            for t in range(2):
                                ak = ("ACC", t)
                                P.op("dve", lambda e, t=t, b=b, N=N: e.tensor_scalar(
                                    out=ACC[:, t, 0:N], in0=U[:, t, b:b + N], scalar1=WCV[:, t, 0:1], scalar2=None, op0=ALU.mult),
                                    ["U", "WCV"], [ak])
                                for jj in (1, 2):
                                    P.op("dve", lambda e, t=t, b=b, N=N, jj=jj: e.scalar_tensor_tensor(
                                        out=ACC[:, t, 0:N], in0=U[:, t, b + jj:b + jj + N], scalar=WCV[:, t, jj:jj + 1],
                                        in1=ACC[:, t, 0:N], op0=ALU.mult, op1=ALU.add),
                                        ["U", "WCV", ak], [ak])
                                P.op("pool", lambda e, t=t, c0=c0, N=N: e.tensor_tensor(
                                    out=YCV[:, t, c0:c0 + N], in0=ACC[:, t, 0:N], in1=ZB[:, t, c0:c0 + N], op=ALU.mult),
                                    [ak, "ZB"], [("YCV", ci)])
                    P.barrier()
                    with ExitStack() as st:
                        KST = sb(st, "KST", [128, 34, 128], BF16)
                        KT = sb(st, "KT", [128, 34 * 128], BF16)
                        VA = sb(st, "VA", [128, 34, 2, 65], BF16)
                        E = sb(st, "E", [128, 3, 1024], BF16)
                        OA = sb(st, "OA", [128, 2, 512])
                        RR = sb(st, "RR", [128, 2, 512])
                        YT = sb(st, "YT", [128, 2, 512], BF16)
                        xo = XOUT[l].ap()
                        P.op("pool", lambda e: e.memset(VA[:], 1.0), [], ["VA"])
                        P.dma("sp", "KST", KST[:, 0:32, :], xo[:, 384:512].rearrange("(t p) c -> p t c", p=128),
                              reads=["XOUT"], writes=["KST"])
                        for hh in range(2):
                            for g in range(2):
                                P.dma("sp", "VA", VA[:, hh * 16:(hh + 1) * 16, g, 0:64],
                                      xo[hh * 2048:(hh + 1) * 2048, g * 64:(g + 1) * 64].rearrange("(t p) d -> p t d", p=128),
                                      reads=["XOUT", "VA"] if (hh == 0 and g == 0) else ["XOUT"], writes=["VA"])
                        P.op("pool", lambda e: e.tensor_copy(out=KST[:, 32:34, :], in_=CKV[:, :, 384:512]), [], ["KST"])
                        P.op("pool", lambda e: e.tensor_copy(
                            out=VA[:, 32:34, :, 0:64], in_=CKV[:, :, 0:128].rearrange("p t (g d) -> p t g d", d=64)), [], ["VA"])
                        for g0 in range(0, 34, 8):
                            n = min(8, 34 - g0)
                            for i in range(n):
                                P.op("pe", lambda e, i=i, g0=g0: e.transpose(
                                    PT[:, i * 128:(i + 1) * 128], KST[:, g0 + i, :], identB[:]), ["KST"], ["PT"])
                            P.op("dve", lambda e, g0=g0, n=n: e.tensor_copy(out=KT[:, g0 * 128:(g0 + n) * 128], in_=PT[:, 0:n * 128]),
                                 ["PT"], ["KT"])
                        spair = [(PA, [("PA", 0), ("PA", 1)]), (PB, [("PB", 0), ("PB", 1)])]
                        it = 0
                        for ci, (c0, N) in enumerate(CH):
                            keys = list(range(34)) if ci < 4 else [32, 33]
                            for j in range(4):
                                for ki, kt in enumerate(keys):
                                    sp_t, sp_k = spair[it % 2]
                                    eb = it % 3
                                    it += 1
                                    for g in range(2):
                                        P.op("pe", lambda e, sp_t=sp_t, g=g, kt=kt, j=j, c0=c0, N=N: e.matmul(
                                            sp_t[:, g * 512:g * 512 + N], KT[g * 64:(g + 1) * 64, kt * 128:(kt + 1) * 128],
                                            QT[g * 64:(g + 1) * 64, j, c0:c0 + N], start=True, stop=True),
                                            ["KT", "QT"], [sp_k[g]])
                                    P.op("act", lambda e, sp_t=sp_t, eb=eb, N=N: e.activation(
                                        out=E[:, eb, :].rearrange("p (g n) -> p g n", g=2)[:, :, 0:N],
                                        in_=sp_t[:, :].rearrange("p (g n) -> p g n", g=2)[:, :, 0:N],
                                        func=AF.Exp, scale=0.125), sp_k, [("E", eb)])
                                    for g in range(2):
                                        ob = C0 if g == 0 else C1
                                        P.op("pe", lambda e, ob=ob, g=g, kt=kt, eb=eb, N=N, ki=ki, nk=len(keys): e.matmul(
                                            ob[0][0:65, 0:N], VA[:, kt, g, :], E[:, eb, g * 512:g * 512 + N],
                                            start=(ki == 0), stop=(ki == nk - 1)),
                                            ["VA", ("E", eb)], [ob[1]])
                                for g in range(2):
                                    ob = C0 if g == 0 else C1
                                    P.op("dve", lambda e, ob=ob, g=g, N=N: e.tensor_copy(out=OA[0:64, g, 0:N], in_=ob[0][0:64, 0:N]),
                                         [ob[1]], [("OA", g)])
                                    P.op("dve", lambda e, ob=ob, g=g, N=N: e.reciprocal(out=RR[64:65, g, 0:N], in_=ob[0][64:65, 0:N]),
                                         [ob[1]], [("RR", g)])
                                    P.op("pe", lambda e, g=g, N=N: e.matmul(S1[0][0:64, 0:N], onesF[64:65, 0:64], RR[64:65, g, 0:N],
                                                                         start=True, stop=True),
                                         [("RR", g), "onesF"], [S1[1]])
                                    if g == 0:
                                        P.op("dve", lambda e, g=g, j=j, c0=c0, N=N: e.tensor_tensor(
                                            out=YATT[0:64, j, c0:c0 + N], in0=OA[0:64, g, 0:N], in1=S1[0][0:64, 0:N], op=ALU.mult),
                                            [("OA", g), S1[1]], [("YATT", ci, j, 0)])
                                    else:
                                        yb = (ci * 4 + j) % 2
                                        P.op("dve", lambda e, g=g, yb=yb, N=N: e.tensor_tensor(
                                            out=YT[0:64, yb, 0:N], in0=OA[0:64, g, 0:N], in1=S1[0][0:64, 0:N], op=ALU.mult),
                                            [("OA", g), S1[1]], [("YT", yb)])
                                        P.dma("sp", "YT%d" % yb, YATT[64:128, j, c0:c0 + N], YT[0:64, yb, 0:N],
                                              reads=[("YT", yb)], writes=[("YATT", ci, j, 1)])
                    P.barrier()
                with ExitStack() as st:
                    DTB = sb(st, "DTB", [128, 2, 2, 4, 512], BF16)
                    ZF = sb(st, "ZF", [128, 2, 4, 256], BF16)
                    AB = sb(st, "AB", [128, 4, 512], BF16)
                    xo = XOUT[l].ap()
                    accs = [A0, A1, B0, B1]
                    it = 0
                    for ci, (c0, N) in enumerate(CH):
                        if ci < 4:
                            for tg in range(8):
                                bf = it % 2
                                it += 1
                                rows = slice(tg * 512, (tg + 1) * 512)
                                P.dma("sp", "DTB%d" % bf, DTB[:, bf, 0], dcos_in[rows, c0:c0 + 512].rearrange("(t p) k -> p t k", p=128),
                                      writes=[("DTB", bf)])
                                P.dma("sp", "DTB%d" % bf, DTB[:, bf, 1], dsin_in[rows, c0:c0 + 512].rearrange("(t p) k -> p t k", p=128),
                                      writes=[("DTB", bf)])
                                P.dma("sp", "ZF%d" % bf, ZF[:, bf], xo[rows, 128:384].rearrange("(t p) c -> p t c", p=128),
                                      reads=["XOUT"], writes=[("ZF", bf)])
                                for t in range(4):
                                    for c2 in range(2):
                                        for ri in range(2):
                                            ak = accs[c2 * 2 + ri]
                                            P.op("pe", lambda e, ak=ak, bf=bf, t=t, c2=c2, ri=ri, tg=tg: e.matmul(
                                                ak[0][:, 0:512], ZF[:, bf, t, c2 * 128:(c2 + 1) * 128], DTB[:, bf, ri, t, :],
                                                start=(tg == 0 and t == 0), stop=(tg == 7 and t == 3)),
                                                [("ZF", bf), ("DTB", bf)], [ak[1]])
                        else:
                            for c2 in range(2):
                                for ri in range(2):
                                    ak = accs[c2 * 2 + ri]
                                    tb = DCC if ri == 0 else DCS
                                    for t in range(2):
                                        P.op("pe", lambda e, ak=ak, tb=tb, t=t, c2=c2: e.matmul(
                                            ak[0][:, 0:256], CKV[:, t, 128 + c2 * 128:256 + c2 * 128], tb[:, t, :],
                                            start=(t == 0), stop=(t == 1)), ["DCC", "DCS"], [ak[1]])
                        for q in range(4):
                            ak = accs[q]
                            if q % 2 == 0:
                                P.op("act", lambda e, ak=ak, q=q, N=N: e.activation(out=AB[:, q, 0:N], in_=ak[0][:, 0:N], func=AF.Copy),
                                     [ak[1]], [("AB", q)])
                            else:
                                P.op("dve", lambda e, ak=ak, q=q, N=N: e.tensor_copy(out=AB[:, q, 0:N], in_=ak[0][:, 0:N]),
                                     [ak[1]], [("AB", q)])
                        for c2 in range(2):
                            ob = C0 if c2 == 0 else C1
                            P.op("pe", lambda e, ob=ob, c2=c2, N=N: e.matmul(ob[0][:, 0:N], CCB[:], AB[:, c2 * 2, 0:N], start=True, stop=False),
                                 [("AB", c2 * 2), "CCB"], [ob[1]])
                            P.op("pe", lambda e, ob=ob, c2=c2, N=N: e.matmul(ob[0][:, 0:N], SCB[:], AB[:, c2 * 2 + 1, 0:N], start=False, stop=True),
                                 [("AB", c2 * 2 + 1), "SCB"], [ob[1]])
                            P.op("dve", lambda e, ob=ob, c2=c2, c0=c0, N=N: e.tensor_copy(out=YFR[:, c2, c0:c0 + N], in_=ob[0][:, 0:N]),
                                 [ob[1]], [("YFR", ci)])
                P.barrier()
                nch = 4 if last else 5
                if debug and l == 0:
                    P.dma("sp", "dbg", dd["d_ycv"], YCV[:], writes=["dbg"])
                    P.dma("sp", "dbg", dd["d_yfr"], YFR[:], writes=["dbg"])
                    P.dma("sp", "dbg", dd["d_yatt"], YATT[:], writes=["dbg"])
                    P.dma("sp", "dbg", dd["d_xout"], XOUT[0].ap(), writes=["dbg"])
                    P.dma("sp", "dbg", dd["d_hg"], HG[:], writes=["dbg"])
                    P.barrier()
                with ExitStack() as st:
                    WG = sb(st, "WG", [128, 8, 3072], BF16)
                    WCO = sb(st, "WCO", [128, 2, D], BF16)
                    WFO = sb(st, "WFO", [128, 2, D], BF16)
                    WAO = sb(st, "WAO", [128, 4, D], BF16)
                    WO = sb(st, "WO", [128, 8, D], BF16)
                    GS = sb(st, "GS", [128, 3, 512], BF16)
                    M1 = sb(st, "M1", [128, 3, 512])
                    MT = sb(st, "MT", [128, 8, 512], BF16)
                    XP = sb(st, "XP", [128, 2, 512])
                    for b in range(3):
                        P.dma("pool", "WG", WG[:, :, b * 1024:(b + 1) * 1024],
                              win_in[l][:, OFF_G + b * 1024: OFF_G + (b + 1) * 1024].rearrange("(k p) n -> p k n", p=128),
                              writes=["WG"])
                    P.dma("pool", "WCO", WCO[:], wco_in[l].rearrange("(k p) n -> p k n", p=128), writes=["WCO"])
                    P.dma("pool", "WFO", WFO[:], wfo_in[l].rearrange("(k p) n -> p k n", p=128), writes=["WFO"])
                    for g in range(2):
                        P.dma("pool", "WAO", WAO[g * 64:(g + 1) * 64, :, :],
                              wao_in[l][g * 256:(g + 1) * 256, :].rearrange("(j d) n -> d j n", d=64), writes=["WAO"])
                    P.dma("pool", "WO", WO[:], wo_in[l].rearrange("(k p) n -> p k n", p=128), writes=["WO"])
                    ycs = lambda ci: [("YCV", ci)]
                    for ci in range(nch):
                        c0, N = CH[ci]
                        s = 1 if ci == 4 else 0
                        for m in range(8):
                            ms = slice(m * 128, (m + 1) * 128)
                            for t in range(2):
                                P.op("pe", lambda e, t=t, ms=ms, c0=c0, N=N: e.matmul(A0[0][:, 0:N], WCO[:, t, ms], YCV[:, t, c0:c0 + N],
                                                                                   start=(t == 0), stop=(t == 1)), ["WCO"], [A0[1]])
                            for t in range(2):
                                P.op("pe", lambda e, t=t, ms=ms, c0=c0, N=N: e.matmul(A1[0][:, 0:N], WFO[:, t, ms], YFR[:, t, c0:c0 + N],
                                                                                   start=(t == 0), stop=(t == 1)), ["WFO"], [A1[1]])
                            for j in range(4):
                                P.op("pe", lambda e, j=j, ms=ms, c0=c0, N=N: e.matmul(B0[0][:, 0:N], WAO[:, j, ms], YATT[:, j, c0:c0 + N],
                                                                                   start=(j == 0), stop=(j == 3)), ["WAO"], [B0[1]])
                            gb = [B1, C0, C1]
                            for b in range(3):
                                for kc in range(8):
                                    P.op("pe", lambda e, b=b, kc=kc, m=m, c0=c0, N=N: e.matmul(
                                        gb[b][0][:, 0:N], WG[:, kc, b * 1024 + m * 128: b * 1024 + (m + 1) * 128],
                                        HT[:, kc, c0:c0 + N], start=(kc == 0), stop=(kc == 7)), ["WG"], [gb[b][1]])
                                P.op("act", lambda e, b=b, N=N: e.activation(out=GS[:, b, 0:N], in_=gb[b][0][:, 0:N], func=AF.Sigmoid),
                                     [gb[b][1]], [("GS", b)])
                            pb = [A0, A1, B0]
                            for b in range(3):
                                P.op("dve", lambda e, b=b, N=N: e.tensor_tensor(out=M1[:, b, 0:N], in0=pb[b][0][:, 0:N], in1=GS[:, b, 0:N], op=ALU.mult),
                                     [pb[b][1], ("GS", b)], [("M1", b)])
                            P.op("pool", lambda e, N=N: e.tensor_tensor(out=M1[:, 0, 0:N], in0=M1[:, 0, 0:N], in1=M1[:, 1, 0:N], op=ALU.add),
                                 [("M1", 0), ("M1", 1)], [("M1", 0)])
                            P.op("pool", lambda e, m=m, N=N: e.tensor_tensor(out=MT[:, m, 0:N], in0=M1[:, 0, 0:N], in1=M1[:, 2, 0:N], op=ALU.add),
                                 [("M1", 0), ("M1", 2)], [("MT", m)])
                        for m2 in range(8):
                            xb = m2 % 2
                            P.dma("sp", "XP%d" % xb, XP[:, xb, 0:N], XD[:, m2, c0:c0 + N], reads=[("XD", ci, m2)], writes=[("XP", xb)])
                            for m in range(8):
                                P.op("pe", lambda e, m=m, m2=m2, N=N: e.matmul(S1[0][:, 0:N], WO[:, m, m2 * 128:(m2 + 1) * 128], MT[:, m, 0:N],
                                                                           start=(m == 0), stop=(m == 7)), ["WO", ("MT", m)], [S1[1]])
                            P.op("dve", lambda e, xb=xb, m2=m2, s=s, N=N: e.scalar_tensor_tensor(
                                out=XP[:, xb, 0:N], in0=S1[0][:, 0:N], scalar=MOD[:, 16 + m2, s:s + 1], in1=XP[:, xb, 0:N],
                                op0=ALU.mult, op1=ALU.add), [S1[1], ("XP", xb)], [("XP", xb)])
                            P.dma("sp", "XP%d" % xb, XD[:, m2, c0:c0 + N], XP[:, xb, 0:N], reads=[("XP", xb)], writes=[("XD", ci, m2)])
                P.barrier()
                if debug and l == 0:
                    P.dma("sp", "dbg", dd["d_xmid"], XD, writes=["dbg"])
                    P.barrier()
                with ExitStack() as st:
                    XC = sb(st, "XC2", [128, 8, 512])
                    SQ = sb(st, "SQb", [128, 8, 512], BF16)
                    RSTD = sb(st, "RSTD2", [128, 512])
                    for ci in range(nch):
                        normalize((XC, SQ, RSTD), ci, G2, "G2", 24, HT)
                P.barrier()
                with ExitStack() as st:
                    W1 = sb(st, "W1", [128, 8, 2048], BF16)
                    W2 = sb(st, "W2", [128, 16, D], BF16)
                    RT = sb(st, "RT", [128, 2, 512], BF16)
                    AT = sb(st, "AT", [128, 16, 512], BF16)
                    XP = sb(st, "XPb", [128, 2, 512])
                    for hf in range(2):
                        for q in range(2):
                            P.dma("pool", "W1", W1[:, :, q * 1024:(q + 1) * 1024],
                                  w1_in[l][:, hf * 2048 + q * 1024: hf * 2048 + (q + 1) * 1024].rearrange("(k p) n -> p k n", p=128),
                                  reads=["W1"] if q == 0 else [], writes=["W1"])
                            P.dma("pool", "W2", W2[:, q * 8:(q + 1) * 8, :],
                                  w2_in[l][hf * 2048 + q * 1024: hf * 2048 + (q + 1) * 1024, :].rearrange("(k p) n -> p k n", p=128),
                                  reads=["W2"] if q == 0 else [], writes=["W2"])
                        for ci in range(nch):
                            c0, N = CH[ci]
                            s = 1 if ci == 4 else 0
                            ub = [A0, A1, B0, B1]
                            for f in range(16):
                                bk = ub[f % 4]
                                rb = f % 2
                                for kc in range(8):
                                    P.op("pe", lambda e, bk=bk, f=f, kc=kc, c0=c0, N=N: e.matmul(
                                        bk[0][:, 0:N], W1[:, kc, f * 128:(f + 1) * 128], HT[:, kc, c0:c0 + N],
                                        start=(kc == 0), stop=(kc == 7)), ["W1", ("HT", ci)], [bk[1]])
                                P.op("act", lambda e, bk=bk, rb=rb, N=N: e.activation(out=RT[:, rb, 0:N], in_=bk[0][:, 0:N], func=AF.Relu),
                                     [bk[1]], [("RT", rb)])
                                sq_eng = "pool" if f % 2 == 0 else "dve"
                                P.op(sq_eng, lambda e, rb=rb, f=f, N=N: e.tensor_tensor(out=AT[:, f, 0:N], in0=RT[:, rb, 0:N], in1=RT[:, rb, 0:N], op=ALU.mult),
                                     [("RT", rb)], [("AT", f)])
                            for m2 in range(8):
                                xb = m2 % 2
                                rk = C0 if m2 % 2 == 0 else C1
                                P.dma("sp", "XQ%d" % xb, XP[:, xb, 0:N], XD[:, m2, c0:c0 + N], reads=[("XD", ci, m2)], writes=[("XPb", xb)])
                                for f in range(16):
                                    P.op("pe", lambda e, rk=rk, f=f, m2=m2, N=N: e.matmul(rk[0][:, 0:N], W2[:, f, m2 * 128:(m2 + 1) * 128], AT[:, f, 0:N],
                                                                                      start=(f == 0), stop=(f == 15)), ["W2", ("AT", f)], [rk[1]])
                                P.op("dve", lambda e, rk=rk, xb=xb, m2=m2, s=s, N=N: e.scalar_tensor_tensor(
                                    out=XP[:, xb, 0:N], in0=rk[0][:, 0:N], scalar=MOD[:, 40 + m2, s:s + 1], in1=XP[:, xb, 0:N],
                                    op0=ALU.mult, op1=ALU.add), [rk[1], ("XPb", xb)], [("XPb", xb)])
                                P.dma("sp", "XQ%d" % xb, XD[:, m2, c0:c0 + N], XP[:, xb, 0:N], reads=[("XPb", xb)], writes=[("XD", ci, m2)])
                P.barrier()
            if debug:
                P.dma("sp", "dbg", dbg_out[l], XD, writes=["dbg"])
                P.barrier()

        with ExitStack() as st:
            XC = sb(st, "XCf", [128, 8, 512])
            YO = sb(st, "YO", [128, 2, D])
            it = 0
            for ci in range(4):
                c0, N = CH[ci]
                P.dma("sp", "XCf", XC[:], XD[:, :, c0:c0 + N], writes=["XCf"])
                for ti in range(4):
                    yb = it % 2
                    pp, pk = [(PA, [("PA", 0), ("PA", 1)]), (PB, [("PB", 0), ("PB", 1)])][it % 2]
                    it += 1
                    for j in range(8):
                        P.op("pe", lambda e, pp=pp, j=j, ti=ti: e.transpose(pp[:, j * 128:(j + 1) * 128], XC[:, j, ti * 128:(ti + 1) * 128], identF[:]),
                             ["XCf"], [pk[j // 4]])
                    P.op("dve", lambda e, pp=pp, yb=yb: e.tensor_copy(out=YO[:, yb, 0:512], in_=pp[:, 0:512]), [pk[0]], [("YO", yb, 0)])
                    P.op("act", lambda e, pp=pp, yb=yb: e.activation(out=YO[:, yb, 512:1024], in_=pp[:, 512:1024], func=AF.Copy), [pk[1]], [("YO", yb, 1)])
                    P.dma("sp", "YO%d" % yb, y_out[c0 + ti * 128: c0 + (ti + 1) * 128, :], YO[:, yb, :],
                          reads=[("YO", yb, 0), ("YO", yb, 1)], writes=["y"])
        P.barrier()
        P.emit()
    return nc


def _consts():
    bf = ml_dtypes.bfloat16
    n = np.arange(4096, dtype=np.int64)
    out = {}
    for h in range(2):
        k = np.arange(h * 2048, (h + 1) * 2048, dtype=np.int64)
        ph = ((n[:, None] * k[None, :]) % 4096).astype(np.float64) * (2 * np.pi / 4096)
        out[("dcos", h)] = (np.cos(ph) / 64.0).astype(np.float32).astype(bf)
        out[("dsin", h)] = (np.sin(ph) / 64.0).astype(np.float32).astype(bf)
        pos = np.arange(h * 2048, (h + 1) * 2048)
        row, col = pos // 64, pos % 64
        inv = 10000.0 ** (-np.arange(16, dtype=np.float32) / 16)
        ang = np.concatenate([row[:, None].astype(np.float32) * inv, col[:, None].astype(np.float32) * inv], axis=-1)
        c = np.cos(ang).astype(np.float32).reshape(16, 128, 32).transpose(1, 0, 2)
        s = np.sin(ang).astype(np.float32).reshape(16, 128, 32).transpose(1, 0, 2)
        out[("rcos", h)] = np.ascontiguousarray(c)
        out[("rsin", h)] = np.ascontiguousarray(s)
        sel = np.zeros((128, 2), np.float32)
        sel[:, 0] = 1.0 if h == 1 else 0.0
        sel[:, 1] = 1.0 if h == 0 else 0.0
        out[("sel", h)] = sel
    m = np.arange(256, dtype=np.int64)
    ph = ((m[:, None] * m[None, :]) % 256).astype(np.float64) * (2 * np.pi / 256)
    out["dcc"] = (np.cos(ph) / 16.0).astype(np.float32).astype(bf)
    out["dcs"] = (np.sin(ph) / 16.0).astype(np.float32).astype(bf)
    c = np.arange(64, dtype=np.int64)
    ph = ((c[:, None] * c[None, :]) % 64).astype(np.float64) * (2 * np.pi / 64)
    cc = np.zeros((128, 128), np.float32)
    sc = np.zeros((128, 128), np.float32)
    for g in range(2):
        cc[g * 64:(g + 1) * 64, g * 64:(g + 1) * 64] = np.cos(ph) / 8.0
        sc[g * 64:(g + 1) * 64, g * 64:(g + 1) * 64] = -np.sin(ph) / 8.0
    out["ccb"] = cc.astype(bf)
    out["scb"] = sc.astype(bf)
    out["ident"] = np.eye(128, dtype=np.float32)
    return out


def make_in_maps(inputs):
    f = lambda a: np.ascontiguousarray(np.asarray(a, dtype=np.float32))
    x, c, ctx, c_ctx = f(inputs["x"]), f(inputs["c"]), f(inputs["ctx"]), f(inputs["c_ctx"])
    K = _consts()
    pt = lambda v, t: np.ascontiguousarray(v.reshape(v.shape[0], t, 128).transpose(0, 2, 1))
    bm = pt(f(inputs["b_mod"]), 48)
    gn1 = pt(f(inputs["g_norm1"]), 8)
    gn2 = pt(f(inputs["g_norm2"]), 8)
    wc = f(inputs["w_conv"])
    wcv = np.ascontiguousarray(wc.reshape(DEPTH, 3, 2, 128).transpose(0, 3, 2, 1))
    gq, gk = f(inputs["g_q"]), f(inputs["g_k"])
    row = np.concatenate([np.tile(gk, (1, 2)), np.tile(gq, (1, 8))], axis=1)
    gqk = np.ascontiguousarray(np.broadcast_to(row[:, None, :], (DEPTH, 128, 640)))
    shared = {
        "w_mod": f(inputs["w_mod"]), "bm": bm, "gn1": gn1, "gn2": gn2, "w_in": f(inputs["w_in"]),
        "wcv": wcv, "gqk": gqk, "w_conv_out": f(inputs["w_conv_out"]), "w_four_out": f(inputs["w_four_out"]),
        "w_attn_out": f(inputs["w_attn_out"]), "w_o": f(inputs["w_o"]), "w_ff1": f(inputs["w_ff1"]),
        "w_ff2": f(inputs["w_ff2"]), "dcc": K["dcc"], "dcs": K["dcs"], "ccb": K["ccb"], "scb": K["scb"],
        "ident": K["ident"],
    }
    maps = []
    for core in range(8):
        b, h = core // 2, core % 2
        cc = np.stack([c[b].reshape(8, 128).T, c_ctx.reshape(8, 128).T], axis=-1)
        m = dict(shared)
        m.update({
            "x": np.ascontiguousarray(x[b, h * 2048:(h + 1) * 2048, :]),
            "ctx": np.ascontiguousarray(ctx[b]),
            "cc": np.ascontiguousarray(cc.astype(np.float32)),
            "rcos": K[("rcos", h)], "rsin": K[("rsin", h)], "dcos": K[("dcos", h)], "dsin": K[("dsin", h)],
            "sel": K[("sel", h)],
        })
        maps.append(m)
    return maps


_NC = {}


def kernel(**inputs):
    if "nc" not in _NC:
        _NC["nc"] = build(debug=True)
    nc = _NC["nc"]
    maps = make_in_maps(inputs)
    res = run_bass_kernel_spmd(nc, maps, core_ids=list(range(8)))
    out = np.empty((4, 4096, D), np.float32)
    for core in range(8):
        b, h = core // 2, core % 2
        out[b, h * 2048:(h + 1) * 2048, :] = res.results[core]["y"]
    return out
```

```python
import types
import numpy as np
import ml_dtypes
from contextlib import ExitStack
import concourse.bass as bass
import concourse.mybir as mybir
from concourse.bass_utils import run_bass_kernel_spmd

F32 = mybir.dt.float32
BF16 = mybir.dt.bfloat16
AF = mybir.ActivationFunctionType
ALU = mybir.AluOpType
AX = mybir.AxisListType
ENGS = ("pe", "act", "dve", "pool", "sp")

DEPTH = 2
D = 1024
NT = 2304
CH = [(0, 512), (512, 512), (1024, 512), (1536, 512), (2048, 256)]
OFF_B, OFF_C, OFF_X, OFF_F, OFF_Q, OFF_K, OFF_V, OFF_G = 0, 256, 512, 768, 1024, 1536, 1664, 1792
EPS = 1e-6


def _freeze(fn):
    if fn is None or fn.__closure__ is None:
        return fn
    cells = []
    for c in fn.__closure__:
        try:
            cells.append(types.CellType(c.cell_contents))
        except ValueError:
            cells.append(c)
    cells = tuple(cells)
    return types.FunctionType(fn.__code__, fn.__globals__, fn.__name__, fn.__defaults__, cells)


class Prog:
    def __init__(self, nc, es):
        self.nc = nc
        self.es = es
        self.ops = {e: [] for e in ENGS}
        self.kw = {}
        self.kr = {}
        self.dsems = {}
        self.esem = {e: es.enter_context(nc.semaphore("es_" + e)) for e in ENGS}
        self.waited = {e: {} for e in ENGS}

    def dsem(self, name):
        if name not in self.dsems:
            h = self.es.enter_context(self.nc.semaphore("ds_" + name))
            self.dsems[name] = [h, 0]
        return self.dsems[name]

    def _filter(self, eng, deps, is_dma):
        out = []
        for ev in deps:
            if ev[0] == "e" and ev[1] == eng and eng == "pe":
                continue
            skey = (ev[0], ev[1])
            if ev[2] <= self.waited[eng].get(skey, -1):
                continue
            self.waited[eng][skey] = ev[2]
            out.append(ev)
        return out

    def _deps(self, eng, reads, writes, is_dma, dma_sem=None):
        deps = []
        for k in reads:
            ev = self.kw.get(k)
            if ev is not None:
                deps.append(ev)
        for k in writes:
            ev = self.kw.get(k)
            rl = self.kr.get(k, [])
            if ev is not None:
                if not (is_dma and ev[0] == "d" and ev[1] == dma_sem and not rl):
                    deps.append(ev)
            deps.extend(rl)
        return self._filter(eng, deps, is_dma)

    def _commit(self, ev, reads, writes):
        for k in reads:
            self.kr.setdefault(k, []).append(ev)
        for k in writes:
            self.kw[k] = ev
            self.kr[k] = []

    def op(self, eng, fn, reads=(), writes=()):
        waits = self._deps(eng, reads, writes, False)
        ev = ("e", eng, len(self.ops[eng]))
        self.ops[eng].append(dict(fn=_freeze(fn), waits=waits, ev=ev, dma=False))
        self._commit(ev, reads, writes)

    def dma(self, eng, sem, out, in_, reads=(), writes=(), **kw):
        s = self.dsem(sem)
        waits = self._deps(eng, reads, writes, True, sem)
        s[1] += 16
        ev = ("d", sem, s[1])
        fn = lambda e, out=out, in_=in_, kw=kw: e.dma_start(out=out, in_=in_, **kw)
        self.ops[eng].append(dict(fn=fn, waits=waits, ev=ev, dma=True, inc=16))
        self._commit(ev, reads, writes)

    def custom(self, eng, sem, inc, fn, reads=(), writes=()):
        s = self.dsem(sem)
        waits = self._deps(eng, reads, writes, True, sem)
        s[1] += inc
        ev = ("d", sem, s[1])
        self.ops[eng].append(dict(fn=_freeze(fn), waits=waits, ev=ev, dma=True, inc=inc))
        self._commit(ev, reads, writes)

    def barrier(self):
        evs = []
        for e in ENGS:
            last = None
            for i in range(len(self.ops[e]) - 1, -1, -1):
                o = self.ops[e][i]
                if o["fn"] is not None and not o["dma"]:
                    last = o["ev"]
                    break
            if last is not None:
                evs.append(last)
        for name, (h, cnt) in self.dsems.items():
            if cnt > 0:
                evs.append(("d", name, cnt))
        for e in ENGS:
            waits = self._filter(e, [ev for ev in evs if not (ev[0] == "e" and ev[1] == e)], False)
            self.ops[e].append(dict(fn=None, waits=waits, ev=None, dma=False))
        self.kw = {}
        self.kr = {}

    def emit(self):
        nc = self.nc
        sig = {e: set() for e in ENGS}
        for e in ENGS:
            for o in self.ops[e]:
                for ev in o["waits"]:
                    if ev[0] == "e":
                        sig[ev[1]].add(ev[2])
        val = {}
        for e in ENGS:
            for r, idx in enumerate(sorted(sig[e])):
                val[(e, idx)] = r + 1
        prog = self

        def replay(ename):
            def body(eng):
                for o in prog.ops[ename]:
                    for ev in o["waits"]:
                        if ev[0] == "e":
                            eng.wait_ge(prog.esem[ev[1]], val[(ev[1], ev[2])])
                        else:
                            eng.wait_ge(prog.dsems[ev[1]][0], ev[2])
                    if o["fn"] is None:
                        continue
                    inst = o["fn"](eng)
                    ev = o["ev"]
                    if o["dma"]:
                        inst.then_inc(prog.dsems[ev[1]][0], o["inc"])
                    elif ev[2] in sig[ename]:
                        inst.then_inc(prog.esem[ename], 1)
            return body

        with nc.Block() as block:
            block.tensor(replay("pe"))
            block.scalar(replay("act"))
            block.vector(replay("dve"))
            block.gpsimd(replay("pool"))
            block.sync(replay("sp"))


def build(debug=False):
    nc = bass.Bass("TRN2", target_bir_lowering=False)
    dt_in = lambda name, shape, dt=F32: nc.dram_tensor(name, list(shape), dt, kind="ExternalInput").ap()
    x_in = dt_in("x", [2048, D])
    ctx_in = dt_in("ctx", [256, D])
    cc_in = dt_in("cc", [128, 8, 2])
    wmod_in = dt_in("w_mod", [DEPTH, D, 6 * D])
    bm_in = dt_in("bm", [DEPTH, 128, 48])
    gn1_in = dt_in("gn1", [DEPTH, 128, 8])
    gn2_in = dt_in("gn2", [DEPTH, 128, 8])
    win_in = dt_in("w_in", [DEPTH, D, 4864])
    wcv_in = dt_in("wcv", [DEPTH, 128, 2, 3])
    gqk_in = dt_in("gqk", [DEPTH, 128, 640])
    wco_in = dt_in("w_conv_out", [DEPTH, 256, D])
    wfo_in = dt_in("w_four_out", [DEPTH, 256, D])
    wao_in = dt_in("w_attn_out", [DEPTH, 512, D])
    wo_in = dt_in("w_o", [DEPTH, D, D])
    w1_in = dt_in("w_ff1", [DEPTH, D, 4 * D])
    w2_in = dt_in("w_ff2", [DEPTH, 4 * D, D])
    rcos_in = dt_in("rcos", [128, 16, 32])
    rsin_in = dt_in("rsin", [128, 16, 32])
    dcos_in = dt_in("dcos", [4096, 2048], BF16)
    dsin_in = dt_in("dsin", [4096, 2048], BF16)
    dcc_in = dt_in("dcc", [256, 256], BF16)
    dcs_in = dt_in("dcs", [256, 256], BF16)
    ccb_in = dt_in("ccb", [128, 128], BF16)
    scb_in = dt_in("scb", [128, 128], BF16)
    sel_in = dt_in("sel", [128, 2])
    ident_in = dt_in("ident", [128, 128])
    y_out = nc.dram_tensor("y", [2048, D], F32, kind="ExternalOutput").ap()
    dbg_out = None
    if debug:
        dbg_out = nc.dram_tensor("dbg", [DEPTH, 128, 8, NT], F32, kind="ExternalOutput").ap()
        dd = {}
        for nm, shp, dt_ in (("d_h", [128, 8, NT], BF16), ("d_ycv", [128, 2, NT], BF16), ("d_yfr", [128, 2, NT], BF16),
                             ("d_yatt", [128, 4, NT], BF16), ("d_xmid", [128, 8, NT], F32), ("d_mod", [128, 48, 2], F32),
                             ("d_qt", [128, 4, NT], BF16), ("d_xout", [4096, 512], BF16), ("d_hg", [128, 2, 8], F32)):
            dd[nm] = nc.dram_tensor(nm, shp, dt_, kind="ExternalOutput").ap()

    XD = nc.dram_tensor("XD", [128, 8, NT], F32).ap()
    XIN = [nc.dram_tensor("XIN%d" % l, [2048, 512], BF16) for l in range(DEPTH)]
    XOUT = [nc.dram_tensor("XOUT%d" % l, [4096, 512], BF16) for l in range(DEPTH)]
    HIN = [nc.dram_tensor("HIN%d" % l, [128, 8], F32) for l in range(DEPTH)]
    HOUT = [nc.dram_tensor("HOUT%d" % l, [256, 8], F32) for l in range(DEPTH)]
    RG = [[0, 1], [2, 3], [4, 5], [6, 7]]

    with ExitStack() as es:
        P = Prog(nc, es)
        cnt = [0]

        def sb(st, name, shape, dt=F32):
            cnt[0] += 1
            return st.enter_context(nc.sbuf_tensor("%s_%d" % (name, cnt[0]), list(shape), dt))
        PA = es.enter_context(nc.psum_tensor("PA", [128, 1024], F32))
        PB = es.enter_context(nc.psum_tensor("PB", [128, 1024], F32))
        PC = es.enter_context(nc.psum_tensor("PC", [128, 1024], F32))
        PS1 = es.enter_context(nc.psum_tensor("PS1", [128, 512], F32))
        PT = es.enter_context(nc.psum_tensor("PT", [128, 1024], BF16))

        def bank(t, name, h):
            return (t[:, h * 512:(h + 1) * 512], (name, h))
        A0, A1 = bank(PA, "PA", 0), bank(PA, "PA", 1)
        B0, B1 = bank(PB, "PB", 0), bank(PB, "PB", 1)
        C0, C1 = bank(PC, "PC", 0), bank(PC, "PC", 1)
        S1 = (PS1[:, :], ("PS1", 0))

        identF = sb(es, "identF", [128, 128])
        identB = sb(es, "identB", [128, 128], BF16)
        onesB = sb(es, "onesB", [128, 128], BF16)
        onesF = sb(es, "onesF", [128, 64])
        RC = sb(es, "RC", [128, 16, 32])
        RS = sb(es, "RS", [128, 16, 32])
        SEL = sb(es, "SEL", [128, 2])
        CCB = sb(es, "CCB", [128, 128], BF16)
        SCB = sb(es, "SCB", [128, 128], BF16)
        DCC = sb(es, "DCC", [128, 2, 256], BF16)
        DCS = sb(es, "DCS", [128, 2, 256], BF16)
        CCF = sb(es, "CCF", [128, 8, 2])
        SCC = sb(es, "SCC", [128, 8, 2], BF16)
        MOD = sb(es, "MOD", [128, 48, 2])
        G1 = sb(es, "G1", [128, 8, 2])
        G2 = sb(es, "G2", [128, 8, 2])
        GN1 = sb(es, "GN1", [128, 8])
        GN2 = sb(es, "GN2", [128, 8])
        BM = sb(es, "BM", [128, 48])
        WCV = sb(es, "WCV", [128, 2, 3])
        GQK = sb(es, "GQK", [128, 640])
        CKV = sb(es, "CKV", [128, 2, 512], BF16)
        HS = sb(es, "HS", [128, 8])
        HG = sb(es, "HG", [128, 2, 8])

        P.dma("sp", "c0", identF[:], ident_in, writes=["identF"])
        P.dma("sp", "c1", RC[:], rcos_in, writes=["RC"])
        P.dma("sp", "c2", RS[:], rsin_in, writes=["RS"])
        P.dma("sp", "c3", SEL[:], sel_in, writes=["SEL"])
        P.dma("sp", "c4", CCB[:], ccb_in, writes=["CCB"])
        P.dma("sp", "c5", SCB[:], scb_in, writes=["SCB"])
        P.dma("sp", "c6", DCC[:], dcc_in.rearrange("(t p) k -> p t k", p=128), writes=["DCC"])
        P.dma("sp", "c7", DCS[:], dcs_in.rearrange("(t p) k -> p t k", p=128), writes=["DCS"])
        P.dma("sp", "c8", CCF[:], cc_in, writes=["CCF"])
        P.op("dve", lambda e: e.tensor_copy(out=identB[:], in_=identF[:]), ["identF"], ["identB"])
        P.op("pool", lambda e: e.memset(onesB[:], 1.0), [], ["onesB"])
        P.op("pool", lambda e: e.memset(onesF[:], 1.0), [], ["onesF"])
        P.op("pool", lambda e: e.memset(HS[:], 0.0), [], ["HS"])
        P.op("act", lambda e: e.activation(out=SCC[:], in_=CCF[:], func=AF.Silu), ["CCF"], ["SCC"])

        with ExitStack() as st:
            XS = sb(st, "XS", [128, 4, D])
            XO = sb(st, "XO", [128, 8, 512])
            for ci, (c0, N) in enumerate(CH):
                nt = N // 128
                if ci < 4:
                    src = x_in[c0:c0 + N, :]
                else:
                    src = ctx_in
                P.dma("sp", "XS", XS[:, 0:nt, :], src.rearrange("(t p) f -> p t f", p=128),
                      writes=["XS"])
                for j in range(8):
                    bk = [A0, A1, B0, B1][j % 4]
                    for ti in range(nt):
                        P.op("pe", lambda e, bk=bk, ti=ti, j=j: e.transpose(
                            bk[0][:, ti * 128:(ti + 1) * 128], XS[:, ti, j * 128:(j + 1) * 128], identF[:]),
                            ["XS", "identF"], [bk[1]])
                    eng = "dve" if j % 2 == 0 else "act"
                    if eng == "dve":
                        P.op("dve", lambda e, bk=bk, j=j, N=N: e.tensor_copy(out=XO[:, j, 0:N], in_=bk[0][:, 0:N]),
                             [bk[1]], [("XO", j)])
                    else:
                        P.op("act", lambda e, bk=bk, j=j, N=N: e.activation(out=XO[:, j, 0:N], in_=bk[0][:, 0:N], func=AF.Copy),
                             [bk[1]], [("XO", j)])
                P.dma("sp", "XO", XD[:, :, c0:c0 + N], XO[:, :, 0:N],
                      reads=[("XO", j) for j in range(8)], writes=[("XD", ci)])
        P.barrier()

        for l in range(DEPTH):
            last = (l == DEPTH - 1)
            P.dma("sp", "p0", BM[:], bm_in[l], writes=["BM"])
            P.dma("sp", "p1", GN1[:], gn1_in[l], writes=["GN1"])
            P.dma("sp", "p2", GN2[:], gn2_in[l], writes=["GN2"])
            P.dma("sp", "p3", WCV[:], wcv_in[l], writes=["WCV"])
            P.dma("sp", "p4", GQK[:], gqk_in[l], writes=["GQK"])
            with ExitStack() as st:
                WM = sb(st, "WM", [128, 2, 8, 512], BF16)
                for pc in range(12):
                    bf = pc % 2
                    P.dma("pool", "WM%d" % bf, WM[:, bf],
                          wmod_in[l][:, pc * 512:(pc + 1) * 512].rearrange("(k p) n -> p k n", p=128),
                          writes=[("WM", bf)])
                    for ct in range(4):
                        col = (pc * 4 + ct) * 2
                        for kc in range(8):
                            P.op("pe", lambda e, bf=bf, ct=ct, kc=kc, col=col: e.matmul(
                                S1[0][:, col:col + 2], WM[:, bf, kc, ct * 128:(ct + 1) * 128], SCC[:, kc, :],
                                start=(kc == 0), stop=(kc == 7)),
                                [("WM", bf), "SCC"], [S1[1]])
                P.op("dve", lambda e: e.tensor_tensor(
                    out=MOD[:], in0=S1[0][:, 0:96].rearrange("p (t s) -> p t s", s=2),
                    in1=BM[:, :].unsqueeze(2).to_broadcast([128, 48, 2]), op=ALU.add),
                    [S1[1], "BM"], ["MOD"])
                for (Gt, GN, o, nm) in ((G1, GN1, 8, "G1"), (G2, GN2, 32, "G2")):
                    P.op("dve", lambda e, Gt=Gt, o=o: e.tensor_scalar(
                        out=Gt[:], in0=MOD[:, o:o + 8, :], scalar1=1.0, scalar2=None, op0=ALU.add),
                        ["MOD"], [nm])
                    P.op("dve", lambda e, Gt=Gt, GN=GN: e.tensor_tensor(
                        out=Gt[:], in0=Gt[:], in1=GN[:, :].unsqueeze(2).to_broadcast([128, 8, 2]), op=ALU.mult),
                        [nm, "GN1", "GN2"], [nm])
            P.barrier()

            def normalize(st_bufs, ci, Gt, Gname, sh_off, HTt, kb=""):
                XC, SQ, RSTD = st_bufs
                kXC, kSQ, kRS = "XC" + kb, "SQ" + kb, "RSTD" + kb
                TMP = XC
                TMP2 = XC
                c0, N = CH[ci]
                s = 1 if ci == 4 else 0
                P.dma("sp", kXC, XC[:, :, 0:N], XD[:, :, c0:c0 + N], reads=[("XD", ci)], writes=[kXC])
                P.op("act", lambda e: e.activation(out=SQ[:, :, 0:N], in_=XC[:, :, 0:N], func=AF.Square),
                     [kXC], [kSQ])
                for j in range(8):
                    P.op("pe", lambda e, j=j: e.matmul(S1[0][:, 0:N], onesB[:], SQ[:, j, 0:N],
                                                      start=(j == 0), stop=(j == 7)),
                         [kSQ, "onesB"], [S1[1]])
                P.op("act", lambda e: e.activation(out=RSTD[:, 0:N], in_=S1[0][:, 0:N], func=AF.Sqrt,
                                                   scale=1.0 / D, bias=EPS),
                     [S1[1]], [kRS])
                P.op("dve", lambda e: e.reciprocal(out=RSTD[:, 0:N], in_=RSTD[:, 0:N]),
                     [kRS], [kRS])
                P.op("dve", lambda e: e.tensor_tensor(
                    out=TMP[:, :, 0:N], in0=XC[:, :, 0:N],
                    in1=RSTD[:, 0:N].unsqueeze(1).to_broadcast([128, 8, N]), op=ALU.mult),
                    [kXC, kRS], [kXC])
                P.op("pool", lambda e: e.tensor_tensor(
                    out=TMP2[:, :, 0:N], in0=TMP[:, :, 0:N],
                    in1=Gt[:, :, s:s + 1].to_broadcast([128, 8, N]), op=ALU.mult),
                    [kXC, Gname], [kXC])
                P.op("dve", lambda e: e.tensor_tensor(
                    out=HTt[:, :, c0:c0 + N], in0=TMP2[:, :, 0:N],
                    in1=MOD[:, sh_off:sh_off + 8, s:s + 1].to_broadcast([128, 8, N]), op=ALU.add),
                    [kXC, "MOD"], [("HT", ci)])

            with ExitStack() as slay:
                HT = sb(slay, "HT", [128, 8, NT], BF16)
                YATT = sb(slay, "YATT", [128, 4, NT], BF16)
                YCV = sb(slay, "YCV", [128, 2, NT], BF16)
                YFR = sb(slay, "YFR", [128, 2, NT], BF16)
                with ExitStack() as sq_:
                    QT = sb(sq_, "QT", [128, 4, NT], BF16)
                    U = sb(sq_, "U", [128, 2, 2308], BF16)
                    ZB = sb(sq_, "ZB", [128, 2, NT], BF16)
                    P.op("pool", lambda e: e.memset(U[:], 0.0), [], ["U"])
                    with ExitStack() as st:
                        WB = sb(st, "WB", [128, 8, 1792], BF16)
                        XC = sb(st, "XC", [128, 8, 512])
                        SQ = sb(st, "SQ", [128, 8, 512], BF16)
                        RSTD = sb(st, "RSTD", [128, 512])
                        CX = sb(st, "CX", [128, 512])
                        SQ2 = sb(st, "SQ2", [128, 2, 640])
                        SS = sb(st, "SS", [128, 2, 10])
                        RQ = sb(st, "RQ", [128, 2, 10])
                        QN = sb(st, "QN", [128, 2, 640])
                        QG = sb(st, "QG", [128, 2, 640])
                        T1 = sb(st, "T1", [128, 2, 320])
                        T2 = sb(st, "T2", [128, 2, 320])
                        T3 = sb(st, "T3", [128, 2, 320])
                        T4 = sb(st, "T4", [128, 2, 320])
                        QR = sb(st, "QR", [128, 2, 640], BF16)
                        XST = sb(st, "XST", [128, 2, 512], BF16)
                        wsrc = win_in[l]
                        def wload(dst0, n, src0, key):
                            P.dma("pool", "WB", WB[:, :, dst0:dst0 + n],
                                  wsrc[:, src0:src0 + n].rearrange("(k p) n -> p k n", p=128), writes=[key])
                        wload(0, 768, OFF_B, "WB")
                        wload(768, 128, OFF_V, "WB")
                        wload(896, 256, OFF_F, "WB")
                        wload(1152, 128, OFF_K, "WB")
                        for g in range(2):
                            for jj in range(4):
                                wload(1280 + jj * 128 + g * 64, 64, OFF_Q + (g * 4 + jj) * 64, "WB")
                        nbufs = (XC, SQ, RSTD)
                        for ci, (c0, N) in enumerate(CH):
                            normalize(nbufs, ci, G1, "G1", 0, HT)
                            for t in range(2):
                                for (bk, off) in ((A0, OFF_B), (A1, OFF_C), (S1, OFF_X)):
                                    for kc in range(8):
                                        P.op("pe", lambda e, bk=bk, off=off, t=t, kc=kc, c0=c0, N=N: e.matmul(
                                            bk[0][:, 0:N], WB[:, kc, off + t * 128: off + (t + 1) * 128],
                                            HT[:, kc, c0:c0 + N], start=(kc == 0), stop=(kc == 7)),
                                            ["WB", ("HT", ci)], [bk[1]])
                                P.op("act", lambda e, N=N: e.activation(out=CX[:, 0:N], in_=S1[0][:, 0:N], func=AF.Copy),
                                     [S1[1]], ["CX"])
                                ub = (1 + c0) if ci < 4 else 2051
                                P.op("dve", lambda e, t=t, ub=ub, N=N: e.tensor_tensor(
                                    out=U[:, t, ub:ub + N], in0=A1[0][:, 0:N], in1=CX[:, 0:N], op=ALU.mult),
                                    [A1[1], "CX"], [("U", ci)])
                                P.op("act", lambda e, t=t, c0=c0, N=N: e.activation(
                                    out=ZB[:, t, c0:c0 + N], in_=A0[0][:, 0:N], func=AF.Copy),
                                    [A0[1]], [("ZB", ci, t)])
                            def tile_gen(ci, c0, ti, si):
                                PP = PC if si == 0 else PB
                                pn = "PC" if si == 0 else "PB"
                                PCk = [(pn, 0), (pn, 1)]
                                K_ = lambda nm: (nm, si)
                                SQ2s, SSs, RQs, QNs, QGs, QRs = SQ2[:, si], SS[:, si], RQ[:, si], QN[:, si], QG[:, si], QR[:, si]
                                T1s, T2s, T3s, T4s = T1[:, si], T2[:, si], T3[:, si], T4[:, si]
                                tg = c0 // 128 + ti
                                t0 = c0 + ti * 128
                                for half in range(2):
                                    for kc in range(8):
                                        P.op("pe", lambda e, half=half, kc=kc: e.matmul(
                                            PP[:, half * 512:(half + 1) * 512], HT[:, kc, t0:t0 + 128],
                                            WB[:, kc, 768 + half * 512: 768 + (half + 1) * 512],
                                            start=(kc == 0), stop=(kc == 7)),
                                            ["WB", ("HT", ci)], [(pn, half)])
                                yield
                                P.op("act", lambda e: e.activation(out=SQ2s, in_=PP[:, 384:1024], func=AF.Square),
                                     PCk, [K_("SQ2")])
                                if ci < 4:
                                    xb = tg % 2
                                    dst = XST[:, xb, :]
                                    dkey = ("XST", xb)
                                else:
                                    dst = CKV[:, ti, :]
                                    dkey = ("CKV", ti)
                                P.op("act", lambda e: e.activation(out=dst[:, 0:384], in_=PP[:, 0:384], func=AF.Copy),
                                     [(pn, 0)], [dkey])
                                yield
                                P.op("dve", lambda e: e.tensor_reduce(
                                    out=SSs, in_=SQ2s.rearrange("p (h d) -> p h d", d=64), axis=AX.X, op=ALU.add),
                                    [K_("SQ2")], [K_("SS")])
                                yield
                                P.op("act", lambda e: e.activation(out=RQs, in_=SSs, func=AF.Sqrt, scale=1.0 / 64, bias=EPS),
                                     [K_("SS")], [K_("RQ")])
                                yield
                                P.op("dve", lambda e: e.reciprocal(out=RQs, in_=RQs), [K_("RQ")], [K_("RQ")])
                                yield
                                P.op("dve", lambda e: e.tensor_tensor(
                                    out=QNs.rearrange("p (h d) -> p h d", d=64),
                                    in0=PP[:, 384:1024].rearrange("p (h d) -> p h d", d=64),
                                    in1=RQs.unsqueeze(2).to_broadcast([128, 10, 64]), op=ALU.mult),
                                    PCk + [K_("RQ")], [K_("QN")])
                                yield
                                P.op("pool", lambda e: e.tensor_tensor(out=QGs, in0=QNs, in1=GQK[:], op=ALU.mult),
                                     [K_("QN"), "GQK"], [K_("QG")])
                                yield
                                QG3 = QGs.rearrange("p (h d) -> p h d", d=64)
                                QR3 = QRs.rearrange("p (h d) -> p h d", d=64)
                                if ci < 4:
                                    cb = RC[:, tg, :].unsqueeze(1).to_broadcast([128, 10, 32])
                                    sbb = RS[:, tg, :].unsqueeze(1).to_broadcast([128, 10, 32])
                                    v3 = lambda T: T.rearrange("p (h d) -> p h d", d=32)
                                    P.op("dve", lambda e: e.tensor_tensor(out=v3(T1s), in0=QG3[:, :, 0:32], in1=cb, op=ALU.mult),
                                         [K_("QG"), "RC"], [K_("T1")])
                                    P.op("pool", lambda e: e.tensor_tensor(out=v3(T2s), in0=QG3[:, :, 32:64], in1=sbb, op=ALU.mult),
                                         [K_("QG"), "RS"], [K_("T2")])
                                    yield
                                    P.op("pool", lambda e: e.tensor_tensor(out=v3(T3s), in0=QG3[:, :, 0:32], in1=sbb, op=ALU.mult),
                                         [K_("QG"), "RS"], [K_("T3")])
                                    P.op("dve", lambda e: e.tensor_tensor(out=v3(T4s), in0=QG3[:, :, 32:64], in1=cb, op=ALU.mult),
                                         [K_("QG"), "RC"], [K_("T4")])
                                    yield
                                    P.op("dve", lambda e: e.tensor_tensor(out=QR3[:, :, 0:32], in0=v3(T1s), in1=v3(T2s), op=ALU.subtract),
                                         [K_("T1"), K_("T2")], [K_("QRa")])
                                    P.op("pool", lambda e: e.tensor_tensor(out=QR3[:, :, 32:64], in0=v3(T3s), in1=v3(T4s), op=ALU.add),
                                         [K_("T3"), K_("T4")], [K_("QRb")])
                                else:
                                    P.op("pool", lambda e: e.tensor_copy(out=QRs, in_=QGs), [K_("QG")], [K_("QRa"), K_("QRb")])
                                yield
                                P.op("pool", lambda e: e.tensor_copy(out=dst[:, 384:512], in_=QRs[:, 0:128]),
                                     [K_("QRa"), K_("QRb")], [(dkey, "k")])
                                if ci < 4:
                                    P.dma("sp", "XST%d" % xb, XIN[l].ap()[t0:t0 + 128, :], dst,
                                          reads=[dkey, (dkey, "k")], writes=["XIN"])
                                for i in range(4):
                                    P.op("pe", lambda e, i=i: e.transpose(
                                        PT[:, i * 128:(i + 1) * 128], QRs[:, 128 + i * 128: 256 + i * 128], identB[:]),
                                        [K_("QRa"), K_("QRb"), "identB"], ["PT"])
                                P.op("dve", lambda e: e.tensor_copy(
                                    out=QT[:, :, t0:t0 + 128], in_=PT[:, 0:512].rearrange("p (j t) -> p j t", t=128)),
                                    ["PT"], [("QT", tg)])
                                yield

                            for tp in range(0, N // 128, 2):
                                gens = [tile_gen(ci, c0, tp + si, si) for si in range(2)]
                                alive = True
                                while alive:
                                    alive = False
                                    for gx in gens:
                                        try:
                                            next(gx)
                                            alive = True
                                        except StopIteration:
                                            pass
                        for t in range(2):
                            P.op("dve", lambda e, t=t: e.tensor_copy(out=HS[:, 2 * t:2 * t + 1], in_=U[:, t, 1:2]), [("U", 0)], ["HS"])
                            P.op("dve", lambda e, t=t: e.tensor_copy(out=HS[:, 2 * t + 1:2 * t + 2], in_=U[:, t, 2048:2049]), [("U", 3)], ["HS"])
                        P.dma("sp", "HS", HIN[l].ap(), HS[:], reads=["HS"], writes=["HIN"])
                    P.barrier()
                    if debug and l == 0:
                        P.dma("sp", "dbg", dd["d_h"], HT[:], writes=["dbg"])
                        P.dma("sp", "dbg", dd["d_qt"], QT[:], writes=["dbg"])
                        P.dma("sp", "dbg", dd["d_mod"], MOD[:], writes=["dbg"])
                    P.custom("pool", "cc", 1, lambda e, l=l: e.collective_compute(
                        "AllGather", ALU.bypass, replica_groups=RG,
                        ins=[XIN[l].ap().opt()], outs=[XOUT[l].ap().opt()]),
                        reads=["XIN"], writes=["XOUT"])
                    P.custom("pool", "cc", 1, lambda e, l=l: e.collective_compute(
                        "AllGather", ALU.bypass, replica_groups=RG,
                        ins=[HIN[l].ap().opt()], outs=[HOUT[l].ap().opt()]),
                        reads=["HIN"], writes=["HOUT"])
                    P.barrier()
                    with ExitStack() as st:
                        ACC = sb(st, "ACC", [128, 2, 512])
                        P.dma("sp", "HG", HG[:], HOUT[l].ap().rearrange("(r p) f -> p r f", p=128),
                              reads=["HOUT"], writes=["HG"])
                        for t in range(2):
                            P.op("dve", lambda e, t=t: e.tensor_tensor(out=U[:, t, 0:1], in0=HG[:, 0, 2 * t + 1:2 * t + 2],
                                                                      in1=SEL[:, 0:1], op=ALU.mult), ["HG", "SEL", "U"], ["U"])
                            P.op("dve", lambda e, t=t: e.tensor_tensor(out=U[:, t, 2049:2050], in0=HG[:, 1, 2 * t:2 * t + 1],
                                                                      in1=SEL[:, 1:2], op=ALU.mult), ["HG", "SEL", "U"], ["U"])
                        for ci, (c0, N) in enumerate(CH):
                            b = c0 if ci < 4 else 2050
                            for t in range(2):
                                ak = ("ACC", t)
                                P.op("dve", lambda e, t=t, b=b, N=N: e.tensor_scalar(
                                    out=ACC[:, t, 0:N], in0=U[:, t, b:b + N], scalar1=WCV[:, t, 0:1], scalar2=None, op0=ALU.mult),
                                    ["U", "WCV"], [ak])
                                for jj in (1, 2):
                                    P.op("dve", lambda e, t=t, b=b, N=N, jj=jj: e.scalar_tensor_tensor(
                                        out=ACC[:, t, 0:N], in0=U[:, t, b + jj:b + jj + N], scalar=WCV[:, t, jj:jj + 1],
                                        in1=ACC[:, t, 0:N], op0=ALU.mult, op1=ALU.add),
                                        ["U", "WCV", ak], [ak])
                                P.op("pool", lambda e, t=t, c0=c0, N=N: e.tensor_tensor(
                                    out=YCV[:, t, c0:c0 + N], in0=ACC[:, t, 0:N], in1=ZB[:, t, c0:c0 + N], op=ALU.mult),
                                    [ak, "ZB"], [("YCV", ci)])
                    P.barrier()
                    with ExitStack() as st:
                        KST = sb(st, "KST", [128, 34, 128], BF16)
                        KT = sb(st, "KT", [128, 34 * 128], BF16)
                        VA = sb(st, "VA", [128, 34, 2, 65], BF16)
                        E = sb(st, "E", [128, 3, 1024], BF16)
                        OA = sb(st, "OA", [128, 2, 512])
                        RR = sb(st, "RR", [128, 2, 512])
                        YT = sb(st, "YT", [128, 2, 512], BF16)
                        xo = XOUT[l].ap()
                        P.op("pool", lambda e: e.memset(VA[:], 1.0), [], ["VA"])
                        P.dma("sp", "KST", KST[:, 0:32, :], xo[:, 384:512].rearrange("(t p) c -> p t c", p=128),
                              reads=["XOUT"], writes=["KST"])
                        for hh in range(2):
                            for g in range(2):
                                P.dma("sp", "VA", VA[:, hh * 16:(hh + 1) * 16, g, 0:64],
                                      xo[hh * 2048:(hh + 1) * 2048, g * 64:(g + 1) * 64].rearrange("(t p) d -> p t d", p=128),
                                      reads=["XOUT", "VA"] if (hh == 0 and g == 0) else ["XOUT"], writes=["VA"])
                        P.op("pool", lambda e: e.tensor_copy(out=KST[:, 32:34, :], in_=CKV[:, :, 384:512]), [], ["KST"])
                        P.op("pool", lambda e: e.tensor_copy(
                            out=VA[:, 32:34, :, 0:64], in_=CKV[:, :, 0:128].rearrange("p t (g d) -> p t g d", d=64)), [], ["VA"])
                        for g0 in range(0, 34, 8):
                            n = min(8, 34 - g0)
                            for i in range(n):
                                P.op("pe", lambda e, i=i, g0=g0: e.transpose(
                                    PT[:, i * 128:(i + 1) * 128], KST[:, g0 + i, :], identB[:]), ["KST"], ["PT"])
                            P.op("dve", lambda e, g0=g0, n=n: e.tensor_copy(out=KT[:, g0 * 128:(g0 + n) * 128], in_=PT[:, 0:n * 128]),
                                 ["PT"], ["KT"])
                        spair = [(PA, [("PA", 0), ("PA", 1)]), (PB, [("PB", 0), ("PB", 1)])]
                        its = []
                        for ci, (c0, N) in enumerate(CH):
                            keys = list(range(34)) if ci < 4 else [32, 33]
                            for j in range(4):
                                for ki, kt in enumerate(keys):
                                    its.append(dict(ci=ci, c0=c0, N=N, j=j, ki=ki, kt=kt, nk=len(keys), n=len(its)))

                        def a_scores(itd):
                            sp_t, sp_k = spair[itd["n"] % 2]
                            kt, j, c0, N = itd["kt"], itd["j"], itd["c0"], itd["N"]
                            for g in range(2):
                                P.op("pe", lambda e, sp_t=sp_t, g=g, kt=kt, j=j, c0=c0, N=N: e.matmul(
                                    sp_t[:, g * 512:g * 512 + N], KT[g * 64:(g + 1) * 64, kt * 128:(kt + 1) * 128],
                                    QT[g * 64:(g + 1) * 64, j, c0:c0 + N], start=True, stop=True),
                                    ["KT", "QT"], [sp_k[g]])

                        def a_exp(itd):
                            sp_t, sp_k = spair[itd["n"] % 2]
                            eb = itd["n"] % 3
                            N = itd["N"]
                            P.op("act", lambda e, sp_t=sp_t, eb=eb, N=N: e.activation(
                                out=E[:, eb, :].rearrange("p (g n) -> p g n", g=2)[:, :, 0:N],
                                in_=sp_t[:, :].rearrange("p (g n) -> p g n", g=2)[:, :, 0:N],
                                func=AF.Exp, scale=0.125), sp_k, [("E", eb)])

                        def a_pv(itd):
                            eb = itd["n"] % 3
                            kt, j, c0, N, ki, nk, ci = itd["kt"], itd["j"], itd["c0"], itd["N"], itd["ki"], itd["nk"], itd["ci"]
                            for g in range(2):
                                ob = C0 if g == 0 else C1
                                P.op("pe", lambda e, ob=ob, g=g, kt=kt, eb=eb, N=N, ki=ki, nk=nk: e.matmul(
                                    ob[0][0:65, 0:N], VA[:, kt, g, :], E[:, eb, g * 512:g * 512 + N],
                                    start=(ki == 0), stop=(ki == nk - 1)),
                                    ["VA", ("E", eb)], [ob[1]])
                            if ki != nk - 1:
                                return
                            for g in range(2):
                                ob = C0 if g == 0 else C1
                                P.op("dve", lambda e, ob=ob, g=g, N=N: e.tensor_copy(out=OA[0:64, g, 0:N], in_=ob[0][0:64, 0:N]),
                                     [ob[1]], [("OA", g)])
                                P.op("dve", lambda e, ob=ob, g=g, N=N: e.reciprocal(out=RR[64:65, g, 0:N], in_=ob[0][64:65, 0:N]),
                                     [ob[1]], [("RR", g)])
                                P.op("pe", lambda e, g=g, N=N: e.matmul(S1[0][0:64, 0:N], onesF[64:65, 0:64], RR[64:65, g, 0:N],
                                                                     start=True, stop=True),
                                     [("RR", g), "onesF"], [S1[1]])
                                if g == 0:
                                    P.op("dve", lambda e, g=g, j=j, c0=c0, N=N: e.tensor_tensor(
                                        out=YATT[0:64, j, c0:c0 + N], in0=OA[0:64, g, 0:N], in1=S1[0][0:64, 0:N], op=ALU.mult),
                                        [("OA", g), S1[1]], [("YATT", ci, j, 0)])
                                else:
                                    yb = (ci * 4 + j) % 2
                                    P.op("dve", lambda e, g=g, yb=yb, N=N: e.tensor_tensor(
                                        out=YT[0:64, yb, 0:N], in0=OA[0:64, g, 0:N], in1=S1[0][0:64, 0:N], op=ALU.mult),
                                        [("OA", g), S1[1]], [("YT", yb)])
                                    P.dma("sp", "YT%d" % yb, YATT[64:128, j, c0:c0 + N], YT[0:64, yb, 0:N],
                                          reads=[("YT", yb)], writes=[("YATT", ci, j, 1)])

                        a_scores(its[0])
                        for n, itd in enumerate(its):
                            if n + 1 < len(its):
                                a_scores(its[n + 1])
                            a_exp(itd)
                            a_pv(itd)
                    P.barrier()
                with ExitStack() as st:
                    DTB = sb(st, "DTB", [128, 2, 2, 4, 512], BF16)
                    ZF = sb(st, "ZF", [128, 2, 4, 256], BF16)
                    AB = sb(st, "AB", [128, 4, 512], BF16)
                    xo = XOUT[l].ap()
                    accs = [A0, A1, B0, B1]
                    it = 0
                    for ci, (c0, N) in enumerate(CH):
                        if ci < 4:
                            for tg in range(8):
                                bf = it % 2
                                it += 1
                                rows = slice(tg * 512, (tg + 1) * 512)
                                P.dma("sp", "DTB%d" % bf, DTB[:, bf, 0], dcos_in[rows, c0:c0 + 512].rearrange("(t p) k -> p t k", p=128),
                                      writes=[("DTB", bf)])
                                P.dma("sp", "DTB%d" % bf, DTB[:, bf, 1], dsin_in[rows, c0:c0 + 512].rearrange("(t p) k -> p t k", p=128),
                                      writes=[("DTB", bf)])
                                P.dma("sp", "ZF%d" % bf, ZF[:, bf], xo[rows, 128:384].rearrange("(t p) c -> p t c", p=128),
                                      reads=["XOUT"], writes=[("ZF", bf)])
                                for t in range(4):
                                    for c2 in range(2):
                                        for ri in range(2):
                                            ak = accs[c2 * 2 + ri]
                                            P.op("pe", lambda e, ak=ak, bf=bf, t=t, c2=c2, ri=ri, tg=tg: e.matmul(
                                                ak[0][:, 0:512], ZF[:, bf, t, c2 * 128:(c2 + 1) * 128], DTB[:, bf, ri, t, :],
                                                start=(tg == 0 and t == 0), stop=(tg == 7 and t == 3)),
                                                [("ZF", bf), ("DTB", bf)], [ak[1]])
                        else:
                            for c2 in range(2):
                                for ri in range(2):
                                    ak = accs[c2 * 2 + ri]
                                    tb = DCC if ri == 0 else DCS
                                    for t in range(2):
                                        P.op("pe", lambda e, ak=ak, tb=tb, t=t, c2=c2: e.matmul(
                                            ak[0][:, 0:256], CKV[:, t, 128 + c2 * 128:256 + c2 * 128], tb[:, t, :],
                                            start=(t == 0), stop=(t == 1)), ["DCC", "DCS"], [ak[1]])
                        for q in range(4):
                            ak = accs[q]
                            if q % 2 == 0:
                                P.op("act", lambda e, ak=ak, q=q, N=N: e.activation(out=AB[:, q, 0:N], in_=ak[0][:, 0:N], func=AF.Copy),
                                     [ak[1]], [("AB", q)])
                            else:
                                P.op("dve", lambda e, ak=ak, q=q, N=N: e.tensor_copy(out=AB[:, q, 0:N], in_=ak[0][:, 0:N]),
                                     [ak[1]], [("AB", q)])
                        for c2 in range(2):
                            ob = C0 if c2 == 0 else C1
                            P.op("pe", lambda e, ob=ob, c2=c2, N=N: e.matmul(ob[0][:, 0:N], CCB[:], AB[:, c2 * 2, 0:N], start=True, stop=False),
                                 [("AB", c2 * 2), "CCB"], [ob[1]])
                            P.op("pe", lambda e, ob=ob, c2=c2, N=N: e.matmul(ob[0][:, 0:N], SCB[:], AB[:, c2 * 2 + 1, 0:N], start=False, stop=True),
                                 [("AB", c2 * 2 + 1), "SCB"], [ob[1]])
                            P.op("dve", lambda e, ob=ob, c2=c2, c0=c0, N=N: e.tensor_copy(out=YFR[:, c2, c0:c0 + N], in_=ob[0][:, 0:N]),
                                 [ob[1]], [("YFR", ci)])
                P.barrier()
                nch = 4 if last else 5
                if debug and l == 0:
                    P.dma("sp", "dbg", dd["d_ycv"], YCV[:], writes=["dbg"])
                    P.dma("sp", "dbg", dd["d_yfr"], YFR[:], writes=["dbg"])
                    P.dma("sp", "dbg", dd["d_yatt"], YATT[:], writes=["dbg"])
                    P.dma("sp", "dbg", dd["d_xout"], XOUT[0].ap(), writes=["dbg"])
                    P.dma("sp", "dbg", dd["d_hg"], HG[:], writes=["dbg"])
                    P.barrier()
                with ExitStack() as st:
                    WG = sb(st, "WG", [128, 8, 3072], BF16)
                    WCO = sb(st, "WCO", [128, 2, D], BF16)
                    WFO = sb(st, "WFO", [128, 2, D], BF16)
                    WAO = sb(st, "WAO", [128, 4, D], BF16)
                    WO = sb(st, "WO", [128, 8, D], BF16)
                    GS = sb(st, "GS", [128, 3, 512], BF16)
                    M1 = sb(st, "M1", [128, 3, 512])
                    MT = sb(st, "MT", [128, 8, 512], BF16)
                    XP = sb(st, "XP", [128, 2, 512])
                    for b in range(3):
                        P.dma("pool", "WG", WG[:, :, b * 1024:(b + 1) * 1024],
                              win_in[l][:, OFF_G + b * 1024: OFF_G + (b + 1) * 1024].rearrange("(k p) n -> p k n", p=128),
                              writes=["WG"])
                    P.dma("pool", "WCO", WCO[:], wco_in[l].rearrange("(k p) n -> p k n", p=128), writes=["WCO"])
                    P.dma("pool", "WFO", WFO[:], wfo_in[l].rearrange("(k p) n -> p k n", p=128), writes=["WFO"])
                    for g in range(2):
                        P.dma("pool", "WAO", WAO[g * 64:(g + 1) * 64, :, :],
                              wao_in[l][g * 256:(g + 1) * 256, :].rearrange("(j d) n -> d j n", d=64), writes=["WAO"])
                    P.dma("pool", "WO", WO[:], wo_in[l].rearrange("(k p) n -> p k n", p=128), writes=["WO"])
                    ycs = lambda ci: [("YCV", ci)]
                    for ci in range(nch):
                        c0, N = CH[ci]
                        s = 1 if ci == 4 else 0
                        for m in range(8):
                            ms = slice(m * 128, (m + 1) * 128)
                            for t in range(2):
                                P.op("pe", lambda e, t=t, ms=ms, c0=c0, N=N: e.matmul(A0[0][:, 0:N], WCO[:, t, ms], YCV[:, t, c0:c0 + N],
                                                                                   start=(t == 0), stop=(t == 1)), ["WCO"], [A0[1]])
                            for t in range(2):
                                P.op("pe", lambda e, t=t, ms=ms, c0=c0, N=N: e.matmul(A1[0][:, 0:N], WFO[:, t, ms], YFR[:, t, c0:c0 + N],
                                                                                   start=(t == 0), stop=(t == 1)), ["WFO"], [A1[1]])
                            for j in range(4):
                                P.op("pe", lambda e, j=j, ms=ms, c0=c0, N=N: e.matmul(B0[0][:, 0:N], WAO[:, j, ms], YATT[:, j, c0:c0 + N],
                                                                                   start=(j == 0), stop=(j == 3)), ["WAO"], [B0[1]])
                            gb = [B1, C0, C1]
                            for b in range(3):
                                for kc in range(8):
                                    P.op("pe", lambda e, b=b, kc=kc, m=m, c0=c0, N=N: e.matmul(
                                        gb[b][0][:, 0:N], WG[:, kc, b * 1024 + m * 128: b * 1024 + (m + 1) * 128],
                                        HT[:, kc, c0:c0 + N], start=(kc == 0), stop=(kc == 7)), ["WG"], [gb[b][1]])
                                P.op("act", lambda e, b=b, N=N: e.activation(out=GS[:, b, 0:N], in_=gb[b][0][:, 0:N], func=AF.Sigmoid),
                                     [gb[b][1]], [("GS", b)])
                            pb = [A0, A1, B0]
                            for b in range(3):
                                P.op("dve", lambda e, b=b, N=N: e.tensor_tensor(out=M1[:, b, 0:N], in0=pb[b][0][:, 0:N], in1=GS[:, b, 0:N], op=ALU.mult),
                                     [pb[b][1], ("GS", b)], [("M1", b)])
                            P.op("pool", lambda e, N=N: e.tensor_tensor(out=M1[:, 0, 0:N], in0=M1[:, 0, 0:N], in1=M1[:, 1, 0:N], op=ALU.add),
                                 [("M1", 0), ("M1", 1)], [("M1", 0)])
                            P.op("pool", lambda e, m=m, N=N: e.tensor_tensor(out=MT[:, m, 0:N], in0=M1[:, 0, 0:N], in1=M1[:, 2, 0:N], op=ALU.add),
                                 [("M1", 0), ("M1", 2)], [("MT", m)])
                        for m2 in range(8):
                            xb = m2 % 2
                            P.dma("sp", "XP%d" % xb, XP[:, xb, 0:N], XD[:, m2, c0:c0 + N], reads=[("XD", ci, m2)], writes=[("XP", xb)])
                            for m in range(8):
                                P.op("pe", lambda e, m=m, m2=m2, N=N: e.matmul(S1[0][:, 0:N], WO[:, m, m2 * 128:(m2 + 1) * 128], MT[:, m, 0:N],
                                                                           start=(m == 0), stop=(m == 7)), ["WO", ("MT", m)], [S1[1]])
                            P.op("dve", lambda e, xb=xb, m2=m2, s=s, N=N: e.scalar_tensor_tensor(
                                out=XP[:, xb, 0:N], in0=S1[0][:, 0:N], scalar=MOD[:, 16 + m2, s:s + 1], in1=XP[:, xb, 0:N],
                                op0=ALU.mult, op1=ALU.add), [S1[1], ("XP", xb)], [("XP", xb)])
                            P.dma("sp", "XP%d" % xb, XD[:, m2, c0:c0 + N], XP[:, xb, 0:N], reads=[("XP", xb)], writes=[("XD", ci, m2)])
                P.barrier()
                if debug and l == 0:
                    P.dma("sp", "dbg", dd["d_xmid"], XD, writes=["dbg"])
                    P.barrier()
                with ExitStack() as st:
                    nb2 = [(sb(st, "XC2", [128, 8, 512]), sb(st, "SQb", [128, 8, 512], BF16), sb(st, "RSTD2", [128, 512]))
                           for _ in range(2)]
                    for ci in range(nch):
                        normalize(nb2[ci % 2], ci, G2, "G2", 24, HT, kb="h%d" % (ci % 2))
                P.barrier()
                with ExitStack() as st:
                    W1 = sb(st, "W1", [128, 8, 2048], BF16)
                    W2 = sb(st, "W2", [128, 16, D], BF16)
                    RT = sb(st, "RT", [128, 2, 512], BF16)
                    AT = sb(st, "AT", [128, 16, 512], BF16)
                    XP = sb(st, "XPb", [128, 2, 512])
                    for hf in range(2):
                        for q in range(2):
                            P.dma("pool", "W1", W1[:, :, q * 1024:(q + 1) * 1024],
                                  w1_in[l][:, hf * 2048 + q * 1024: hf * 2048 + (q + 1) * 1024].rearrange("(k p) n -> p k n", p=128),
                                  reads=["W1"] if q == 0 else [], writes=["W1"])
                            P.dma("pool", "W2", W2[:, q * 8:(q + 1) * 8, :],
                                  w2_in[l][hf * 2048 + q * 1024: hf * 2048 + (q + 1) * 1024, :].rearrange("(k p) n -> p k n", p=128),
                                  reads=["W2"] if q == 0 else [], writes=["W2"])
                        for ci in range(nch):
                            c0, N = CH[ci]
                            s = 1 if ci == 4 else 0
                            ub = [A0, A1, B0, B1]
                            for f in range(16):
                                bk = ub[f % 4]
                                rb = f % 2
                                for kc in range(8):
                                    P.op("pe", lambda e, bk=bk, f=f, kc=kc, c0=c0, N=N: e.matmul(
                                        bk[0][:, 0:N], W1[:, kc, f * 128:(f + 1) * 128], HT[:, kc, c0:c0 + N],
                                        start=(kc == 0), stop=(kc == 7)), ["W1", ("HT", ci)], [bk[1]])
                                P.op("act", lambda e, bk=bk, rb=rb, N=N: e.activation(out=RT[:, rb, 0:N], in_=bk[0][:, 0:N], func=AF.Relu),
                                     [bk[1]], [("RT", rb)])
                                sq_eng = "pool" if f % 2 == 0 else "dve"
                                P.op(sq_eng, lambda e, rb=rb, f=f, N=N: e.tensor_tensor(out=AT[:, f, 0:N], in0=RT[:, rb, 0:N], in1=RT[:, rb, 0:N], op=ALU.mult),
                                     [("RT", rb)], [("AT", f)])
                            for m2 in range(8):
                                xb = m2 % 2
                                rk = C0 if m2 % 2 == 0 else C1
                                P.dma("sp", "XQ%d" % xb, XP[:, xb, 0:N], XD[:, m2, c0:c0 + N], reads=[("XD", ci, m2)], writes=[("XPb", xb)])
                                for f in range(16):
                                    P.op("pe", lambda e, rk=rk, f=f, m2=m2, N=N: e.matmul(rk[0][:, 0:N], W2[:, f, m2 * 128:(m2 + 1) * 128], AT[:, f, 0:N],
                                                                                      start=(f == 0), stop=(f == 15)), ["W2", ("AT", f)], [rk[1]])
                                P.op("dve", lambda e, rk=rk, xb=xb, m2=m2, s=s, N=N: e.scalar_tensor_tensor(
                                    out=XP[:, xb, 0:N], in0=rk[0][:, 0:N], scalar=MOD[:, 40 + m2, s:s + 1], in1=XP[:, xb, 0:N],
                                    op0=ALU.mult, op1=ALU.add), [rk[1], ("XPb", xb)], [("XPb", xb)])
                                P.dma("sp", "XQ%d" % xb, XD[:, m2, c0:c0 + N], XP[:, xb, 0:N], reads=[("XPb", xb)], writes=[("XD", ci, m2)])
                P.barrier()
            if debug:
                P.dma("sp", "dbg", dbg_out[l], XD, writes=["dbg"])
                P.barrier()

        with ExitStack() as st:
            XC = sb(st, "XCf", [128, 8, 512])
            YO = sb(st, "YO", [128, 2, D])
            it = 0
            for ci in range(4):
                c0, N = CH[ci]
                P.dma("sp", "XCf", XC[:], XD[:, :, c0:c0 + N], writes=["XCf"])
                for ti in range(4):
                    yb = it % 2
                    pp, pk = [(PA, [("PA", 0), ("PA", 1)]), (PB, [("PB", 0), ("PB", 1)])][it % 2]
                    it += 1
                    for j in range(8):
                        P.op("pe", lambda e, pp=pp, j=j, ti=ti: e.transpose(pp[:, j * 128:(j + 1) * 128], XC[:, j, ti * 128:(ti + 1) * 128], identF[:]),
                             ["XCf"], [pk[j // 4]])
                    P.op("dve", lambda e, pp=pp, yb=yb: e.tensor_copy(out=YO[:, yb, 0:512], in_=pp[:, 0:512]), [pk[0]], [("YO", yb, 0)])
                    P.op("act", lambda e, pp=pp, yb=yb: e.activation(out=YO[:, yb, 512:1024], in_=pp[:, 512:1024], func=AF.Copy), [pk[1]], [("YO", yb, 1)])
                    P.dma("sp", "YO%d" % yb, y_out[c0 + ti * 128: c0 + (ti + 1) * 128, :], YO[:, yb, :],
                          reads=[("YO", yb, 0), ("YO", yb, 1)], writes=["y"])
        P.barrier()
        P.emit()
    return nc


def _consts():
    bf = ml_dtypes.bfloat16
    n = np.arange(4096, dtype=np.int64)
    out = {}
    for h in range(2):
        k = np.arange(h * 2048, (h + 1) * 2048, dtype=np.int64)
        ph = ((n[:, None] * k[None, :]) % 4096).astype(np.float64) * (2 * np.pi / 4096)
        out[("dcos", h)] = (np.cos(ph) / 64.0).astype(np.float32).astype(bf)
        out[("dsin", h)] = (np.sin(ph) / 64.0).astype(np.float32).astype(bf)
        pos = np.arange(h * 2048, (h + 1) * 2048)
        row, col = pos // 64, pos % 64
        inv = 10000.0 ** (-np.arange(16, dtype=np.float32) / 16)
        ang = np.concatenate([row[:, None].astype(np.float32) * inv, col[:, None].astype(np.float32) * inv], axis=-1)
        c = np.cos(ang).astype(np.float32).reshape(16, 128, 32).transpose(1, 0, 2)
        s = np.sin(ang).astype(np.float32).reshape(16, 128, 32).transpose(1, 0, 2)
        out[("rcos", h)] = np.ascontiguousarray(c)
        out[("rsin", h)] = np.ascontiguousarray(s)
        sel = np.zeros((128, 2), np.float32)
        sel[:, 0] = 1.0 if h == 1 else 0.0
        sel[:, 1] = 1.0 if h == 0 else 0.0
        out[("sel", h)] = sel
    m = np.arange(256, dtype=np.int64)
    ph = ((m[:, None] * m[None, :]) % 256).astype(np.float64) * (2 * np.pi / 256)
    out["dcc"] = (np.cos(ph) / 16.0).astype(np.float32).astype(bf)
    out["dcs"] = (np.sin(ph) / 16.0).astype(np.float32).astype(bf)
    c = np.arange(64, dtype=np.int64)
    ph = ((c[:, None] * c[None, :]) % 64).astype(np.float64) * (2 * np.pi / 64)
    cc = np.zeros((128, 128), np.float32)
    sc = np.zeros((128, 128), np.float32)
    for g in range(2):
        cc[g * 64:(g + 1) * 64, g * 64:(g + 1) * 64] = np.cos(ph) / 8.0
        sc[g * 64:(g + 1) * 64, g * 64:(g + 1) * 64] = -np.sin(ph) / 8.0
    out["ccb"] = cc.astype(bf)
    out["scb"] = sc.astype(bf)
    out["ident"] = np.eye(128, dtype=np.float32)
    return out


def make_in_maps(inputs):
    f = lambda a: np.ascontiguousarray(np.asarray(a, dtype=np.float32))
    x, c, ctx, c_ctx = f(inputs["x"]), f(inputs["c"]), f(inputs["ctx"]), f(inputs["c_ctx"])
    K = _consts()
    pt = lambda v, t: np.ascontiguousarray(v.reshape(v.shape[0], t, 128).transpose(0, 2, 1))
    bm = pt(f(inputs["b_mod"]), 48)
    gn1 = pt(f(inputs["g_norm1"]), 8)
    gn2 = pt(f(inputs["g_norm2"]), 8)
    wc = f(inputs["w_conv"])
    wcv = np.ascontiguousarray(wc.reshape(DEPTH, 3, 2, 128).transpose(0, 3, 2, 1))
    gq, gk = f(inputs["g_q"]), f(inputs["g_k"])
    row = np.concatenate([np.tile(gk, (1, 2)), np.tile(gq, (1, 8))], axis=1)
    gqk = np.ascontiguousarray(np.broadcast_to(row[:, None, :], (DEPTH, 128, 640)))
    shared = {
        "w_mod": f(inputs["w_mod"]), "bm": bm, "gn1": gn1, "gn2": gn2, "w_in": f(inputs["w_in"]),
        "wcv": wcv, "gqk": gqk, "w_conv_out": f(inputs["w_conv_out"]), "w_four_out": f(inputs["w_four_out"]),
        "w_attn_out": f(inputs["w_attn_out"]), "w_o": f(inputs["w_o"]), "w_ff1": f(inputs["w_ff1"]),
        "w_ff2": f(inputs["w_ff2"]), "dcc": K["dcc"], "dcs": K["dcs"], "ccb": K["ccb"], "scb": K["scb"],
        "ident": K["ident"],
    }
    maps = []
    for core in range(8):
        b, h = core // 2, core % 2
        cc = np.stack([c[b].reshape(8, 128).T, c_ctx.reshape(8, 128).T], axis=-1)
        m = dict(shared)
        m.update({
            "x": np.ascontiguousarray(x[b, h * 2048:(h + 1) * 2048, :]),
            "ctx": np.ascontiguousarray(ctx[b]),
            "cc": np.ascontiguousarray(cc.astype(np.float32)),
            "rcos": K[("rcos", h)], "rsin": K[("rsin", h)], "dcos": K[("dcos", h)], "dsin": K[("dsin", h)],
            "sel": K[("sel", h)],
        })
        maps.append(m)
    return maps


_NC = {}


def kernel(**inputs):
    if "nc" not in _NC:
        _NC["nc"] = build(debug=True)
    nc = _NC["nc"]
    maps = make_in_maps(inputs)
    res = run_bass_kernel_spmd(nc, maps, core_ids=list(range(8)))
    out = np.empty((4, 4096, D), np.float32)
    for core in range(8):
        b, h = core // 2, core % 2
        out[b, h * 2048:(h + 1) * 2048, :] = res.results[core]["y"]
    return out
```

```python
import types
import numpy as np
import ml_dtypes
from contextlib import ExitStack
import concourse.bass as bass
import concourse.mybir as mybir
from concourse.bass_utils import run_bass_kernel_spmd

F32 = mybir.dt.float32
BF16 = mybir.dt.bfloat16
AF = mybir.ActivationFunctionType
ALU = mybir.AluOpType
AX = mybir.AxisListType
ENGS = ("pe", "act", "dve", "pool", "sp")

DEPTH = 2
D = 1024
NT = 2304
CH = [(0, 512), (512, 512), (1024, 512), (1536, 512), (2048, 256)]
OFF_B, OFF_C, OFF_X, OFF_F, OFF_Q, OFF_K, OFF_V, OFF_G = 0, 256, 512, 768, 1024, 1536, 1664, 1792
EPS = 1e-6


def _freeze(fn):
    if fn is None or fn.__closure__ is None:
        return fn
    cells = []
    for c in fn.__closure__:
        try:
            cells.append(types.CellType(c.cell_contents))
        except ValueError:
            cells.append(c)
    cells = tuple(cells)
    return types.FunctionType(fn.__code__, fn.__globals__, fn.__name__, fn.__defaults__, cells)


class Prog:
    def __init__(self, nc, es):
        self.nc = nc
        self.es = es
        self.ops = {e: [] for e in ENGS}
        self.kw = {}
        self.kr = {}
        self.dsems = {}
        self.esem = {e: es.enter_context(nc.semaphore("es_" + e)) for e in ENGS}
        self.waited = {e: {} for e in ENGS}

    def dsem(self, name):
        if name not in self.dsems:
            h = self.es.enter_context(self.nc.semaphore("ds_" + name))
            self.dsems[name] = [h, 0]
        return self.dsems[name]

    def _filter(self, eng, deps, is_dma):
        out = []
        for ev in deps:
            if ev[0] == "e" and ev[1] == eng and eng == "pe":
                continue
            skey = (ev[0], ev[1])
            if ev[2] <= self.waited[eng].get(skey, -1):
                continue
            self.waited[eng][skey] = ev[2]
            out.append(ev)
        return out

    def _deps(self, eng, reads, writes, is_dma, dma_sem=None):
        deps = []
        for k in reads:
            ev = self.kw.get(k)
            if ev is not None:
                deps.append(ev)
        for k in writes:
            ev = self.kw.get(k)
            rl = self.kr.get(k, [])
            if ev is not None:
                if not (is_dma and ev[0] == "d" and ev[1] == dma_sem and not rl):
                    deps.append(ev)
            deps.extend(rl)
        return self._filter(eng, deps, is_dma)

    def _commit(self, ev, reads, writes):
        for k in reads:
            self.kr.setdefault(k, []).append(ev)
        for k in writes:
            self.kw[k] = ev
            self.kr[k] = []

    def op(self, eng, fn, reads=(), writes=()):
        waits = self._deps(eng, reads, writes, False)
        ev = ("e", eng, len(self.ops[eng]))
        self.ops[eng].append(dict(fn=_freeze(fn), waits=waits, ev=ev, dma=False))
        self._commit(ev, reads, writes)

    def dma(self, eng, sem, out, in_, reads=(), writes=(), **kw):
        s = self.dsem(sem)
        waits = self._deps(eng, reads, writes, True, sem)
        s[1] += 16
        ev = ("d", sem, s[1])
        fn = lambda e, out=out, in_=in_, kw=kw: e.dma_start(out=out, in_=in_, **kw)
        self.ops[eng].append(dict(fn=fn, waits=waits, ev=ev, dma=True, inc=16))
        self._commit(ev, reads, writes)

    def custom(self, eng, sem, inc, fn, reads=(), writes=()):
        s = self.dsem(sem)
        waits = self._deps(eng, reads, writes, True, sem)
        s[1] += inc
        ev = ("d", sem, s[1])
        self.ops[eng].append(dict(fn=_freeze(fn), waits=waits, ev=ev, dma=True, inc=inc))
        self._commit(ev, reads, writes)

    def barrier(self):
        evs = []
        for e in ENGS:
            last = None
            for i in range(len(self.ops[e]) - 1, -1, -1):
                o = self.ops[e][i]
                if o["fn"] is not None and not o["dma"]:
                    last = o["ev"]
                    break
            if last is not None:
                evs.append(last)
        for name, (h, cnt) in self.dsems.items():
            if cnt > 0:
                evs.append(("d", name, cnt))
        for e in ENGS:
            waits = self._filter(e, [ev for ev in evs if not (ev[0] == "e" and ev[1] == e)], False)
            self.ops[e].append(dict(fn=None, waits=waits, ev=None, dma=False))
        self.kw = {}
        self.kr = {}

    def emit(self):
        nc = self.nc
        sig = {e: set() for e in ENGS}
        for e in ENGS:
            for o in self.ops[e]:
                for ev in o["waits"]:
                    if ev[0] == "e":
                        sig[ev[1]].add(ev[2])
        val = {}
        for e in ENGS:
            for r, idx in enumerate(sorted(sig[e])):
                val[(e, idx)] = r + 1
        prog = self

        def replay(ename):
            def body(eng):
                for o in prog.ops[ename]:
                    for ev in o["waits"]:
                        if ev[0] == "e":
                            eng.wait_ge(prog.esem[ev[1]], val[(ev[1], ev[2])])
                        else:
                            eng.wait_ge(prog.dsems[ev[1]][0], ev[2])
                    if o["fn"] is None:
                        continue
                    inst = o["fn"](eng)
                    ev = o["ev"]
                    if o["dma"]:
                        inst.then_inc(prog.dsems[ev[1]][0], o["inc"])
                    elif ev[2] in sig[ename]:
                        inst.then_inc(prog.esem[ename], 1)
            return body

        with nc.Block() as block:
            block.tensor(replay("pe"))
            block.scalar(replay("act"))
            block.vector(replay("dve"))
            block.gpsimd(replay("pool"))
            block.sync(replay("sp"))


def build(debug=False):
    nc = bass.Bass("TRN2", target_bir_lowering=False)
    dt_in = lambda name, shape, dt=F32: nc.dram_tensor(name, list(shape), dt, kind="ExternalInput").ap()
    x_in = dt_in("x", [2048, D])
    ctx_in = dt_in("ctx", [256, D])
    cc_in = dt_in("cc", [128, 8, 2])
    wmod_in = dt_in("w_mod", [DEPTH, D, 6 * D])
    bm_in = dt_in("bm", [DEPTH, 128, 48])
    gn1_in = dt_in("gn1", [DEPTH, 128, 8])
    gn2_in = dt_in("gn2", [DEPTH, 128, 8])
    win_in = dt_in("w_in", [DEPTH, D, 4864])
    wcv_in = dt_in("wcv", [DEPTH, 128, 2, 3])
    gqk_in = dt_in("gqk", [DEPTH, 128, 640])
    wco_in = dt_in("w_conv_out", [DEPTH, 256, D])
    wfo_in = dt_in("w_four_out", [DEPTH, 256, D])
    wao_in = dt_in("w_attn_out", [DEPTH, 512, D])
    wo_in = dt_in("w_o", [DEPTH, D, D])
    w1_in = dt_in("w_ff1", [DEPTH, D, 4 * D])
    w2_in = dt_in("w_ff2", [DEPTH, 4 * D, D])
    rcos_in = dt_in("rcos", [128, 16, 32])
    rsin_in = dt_in("rsin", [128, 16, 32])
    dcos_in = dt_in("dcos", [2048, 2048], BF16)
    dsin_in = dt_in("dsin", [2048, 2048], BF16)
    dcc_in = dt_in("dcc", [256, 256], BF16)
    dcs_in = dt_in("dcs", [256, 256], BF16)
    ccb_in = dt_in("ccb", [128, 128], BF16)
    scb_in = dt_in("scb", [128, 128], BF16)
    sel_in = dt_in("sel", [128, 2])
    ident_in = dt_in("ident", [128, 128])
    y_out = nc.dram_tensor("y", [2048, D], F32, kind="ExternalOutput").ap()
    dbg_out = None
    if debug:
        dbg_out = nc.dram_tensor("dbg", [DEPTH, 128, 8, NT], F32, kind="ExternalOutput").ap()
        dd = {}
        for nm, shp, dt_ in (("d_h", [128, 8, NT], BF16), ("d_ycv", [128, 2, NT], BF16), ("d_yfr", [128, 2, NT], BF16),
                             ("d_yatt", [128, 4, NT], BF16), ("d_xmid", [128, 8, NT], F32), ("d_mod", [128, 48, 2], F32),
                             ("d_qt", [128, 4, NT], BF16), ("d_xout", [4096, 512], BF16), ("d_hg", [128, 2, 8], F32)):
            dd[nm] = nc.dram_tensor(nm, shp, dt_, kind="ExternalOutput").ap()

    XD = nc.dram_tensor("XD", [128, 8, NT], F32).ap()
    XIN = [nc.dram_tensor("XIN%d" % l, [2048, 512], BF16) for l in range(DEPTH)]
    XOUT = [nc.dram_tensor("XOUT%d" % l, [4096, 512], BF16) for l in range(DEPTH)]
    HIN = [nc.dram_tensor("HIN%d" % l, [128, 8], F32) for l in range(DEPTH)]
    HOUT = [nc.dram_tensor("HOUT%d" % l, [256, 8], F32) for l in range(DEPTH)]
    RG = [[0, 1], [2, 3], [4, 5], [6, 7]]

    with ExitStack() as es:
        P = Prog(nc, es)
        cnt = [0]

        def sb(st, name, shape, dt=F32):
            cnt[0] += 1
            return st.enter_context(nc.sbuf_tensor("%s_%d" % (name, cnt[0]), list(shape), dt))
        PA = es.enter_context(nc.psum_tensor("PA", [128, 1024], F32))
        PB = es.enter_context(nc.psum_tensor("PB", [128, 1024], F32))
        PC = es.enter_context(nc.psum_tensor("PC", [128, 1024], F32))
        PS1 = es.enter_context(nc.psum_tensor("PS1", [128, 512], F32))
        PT = es.enter_context(nc.psum_tensor("PT", [128, 1024], BF16))

        def bank(t, name, h):
            return (t[:, h * 512:(h + 1) * 512], (name, h))
        A0, A1 = bank(PA, "PA", 0), bank(PA, "PA", 1)
        B0, B1 = bank(PB, "PB", 0), bank(PB, "PB", 1)
        C0, C1 = bank(PC, "PC", 0), bank(PC, "PC", 1)
        S1 = (PS1[:, :], ("PS1", 0))

        identF = sb(es, "identF", [128, 128])
        identB = sb(es, "identB", [128, 128], BF16)
        onesB = sb(es, "onesB", [128, 128], BF16)
        onesF = sb(es, "onesF", [128, 64])
        RC = sb(es, "RC", [128, 16, 32])
        RS = sb(es, "RS", [128, 16, 32])
        SEL = sb(es, "SEL", [128, 2])
        CCB = sb(es, "CCB", [128, 128], BF16)
        SCB = sb(es, "SCB", [128, 128], BF16)
        DCC = sb(es, "DCC", [128, 2, 256], BF16)
        DCS = sb(es, "DCS", [128, 2, 256], BF16)
        CCF = sb(es, "CCF", [128, 8, 2])
        SCC = sb(es, "SCC", [128, 8, 2], BF16)
        MOD = sb(es, "MOD", [128, 48, 2])
        G1 = sb(es, "G1", [128, 8, 2])
        G2 = sb(es, "G2", [128, 8, 2])
        GN1 = sb(es, "GN1", [128, 8])
        GN2 = sb(es, "GN2", [128, 8])
        BM = sb(es, "BM", [128, 48])
        WCV = sb(es, "WCV", [128, 2, 3])
        GQK = sb(es, "GQK", [128, 640])
        CKV = sb(es, "CKV", [128, 2, 512], BF16)
        HS = sb(es, "HS", [128, 8])
        HG = sb(es, "HG", [128, 2, 8])

        P.dma("sp", "c0", identF[:], ident_in, writes=["identF"])
        P.dma("sp", "c1", RC[:], rcos_in, writes=["RC"])
        P.dma("sp", "c2", RS[:], rsin_in, writes=["RS"])
        P.dma("sp", "c3", SEL[:], sel_in, writes=["SEL"])
        P.dma("sp", "c4", CCB[:], ccb_in, writes=["CCB"])
        P.dma("sp", "c5", SCB[:], scb_in, writes=["SCB"])
        P.dma("sp", "c6", DCC[:], dcc_in.rearrange("(t p) k -> p t k", p=128), writes=["DCC"])
        P.dma("sp", "c7", DCS[:], dcs_in.rearrange("(t p) k -> p t k", p=128), writes=["DCS"])
        P.dma("sp", "c8", CCF[:], cc_in, writes=["CCF"])
        P.op("dve", lambda e: e.tensor_copy(out=identB[:], in_=identF[:]), ["identF"], ["identB"])
        P.op("pool", lambda e: e.memset(onesB[:], 1.0), [], ["onesB"])
        P.op("pool", lambda e: e.memset(onesF[:], 1.0), [], ["onesF"])
        P.op("pool", lambda e: e.memset(HS[:], 0.0), [], ["HS"])
        P.op("act", lambda e: e.activation(out=SCC[:], in_=CCF[:], func=AF.Silu), ["CCF"], ["SCC"])

        with ExitStack() as st:
            XS = sb(st, "XS", [128, 4, D])
            XO = sb(st, "XO", [128, 8, 512])
            for ci, (c0, N) in enumerate(CH):
                nt = N // 128
                if ci < 4:
                    src = x_in[c0:c0 + N, :]
                else:
                    src = ctx_in
                P.dma("sp", "XS", XS[:, 0:nt, :], src.rearrange("(t p) f -> p t f", p=128),
                      writes=["XS"])
                for j in range(8):
                    bk = [A0, A1, B0, B1][j % 4]
                    for ti in range(nt):
                        P.op("pe", lambda e, bk=bk, ti=ti, j=j: e.transpose(
                            bk[0][:, ti * 128:(ti + 1) * 128], XS[:, ti, j * 128:(j + 1) * 128], identF[:]),
                            ["XS", "identF"], [bk[1]])
                    eng = "dve" if j % 2 == 0 else "act"
                    if eng == "dve":
                        P.op("dve", lambda e, bk=bk, j=j, N=N: e.tensor_copy(out=XO[:, j, 0:N], in_=bk[0][:, 0:N]),
                             [bk[1]], [("XO", j)])
                    else:
                        P.op("act", lambda e, bk=bk, j=j, N=N: e.activation(out=XO[:, j, 0:N], in_=bk[0][:, 0:N], func=AF.Copy),
                             [bk[1]], [("XO", j)])
                P.dma("sp", "XO", XD[:, :, c0:c0 + N], XO[:, :, 0:N],
                      reads=[("XO", j) for j in range(8)], writes=[("XD", ci)])
        P.barrier()

        for l in range(DEPTH):
            last = (l == DEPTH - 1)
            P.dma("sp", "p0", BM[:], bm_in[l], writes=["BM"])
            P.dma("sp", "p1", GN1[:], gn1_in[l], writes=["GN1"])
            P.dma("sp", "p2", GN2[:], gn2_in[l], writes=["GN2"])
            P.dma("sp", "p3", WCV[:], wcv_in[l], writes=["WCV"])
            P.dma("sp", "p4", GQK[:], gqk_in[l], writes=["GQK"])
            with ExitStack() as st:
                WM = sb(st, "WM", [128, 2, 8, 512], BF16)
                for pc in range(12):
                    bf = pc % 2
                    P.dma("pool", "WM%d" % bf, WM[:, bf],
                          wmod_in[l][:, pc * 512:(pc + 1) * 512].rearrange("(k p) n -> p k n", p=128),
                          writes=[("WM", bf)])
                    for ct in range(4):
                        col = (pc * 4 + ct) * 2
                        for kc in range(8):
                            P.op("pe", lambda e, bf=bf, ct=ct, kc=kc, col=col: e.matmul(
                                S1[0][:, col:col + 2], WM[:, bf, kc, ct * 128:(ct + 1) * 128], SCC[:, kc, :],
                                start=(kc == 0), stop=(kc == 7)),
                                [("WM", bf), "SCC"], [S1[1]])
                P.op("dve", lambda e: e.tensor_tensor(
                    out=MOD[:], in0=S1[0][:, 0:96].rearrange("p (t s) -> p t s", s=2),
                    in1=BM[:, :].unsqueeze(2).to_broadcast([128, 48, 2]), op=ALU.add),
                    [S1[1], "BM"], ["MOD"])
                for (Gt, GN, o, nm) in ((G1, GN1, 8, "G1"), (G2, GN2, 32, "G2")):
                    P.op("dve", lambda e, Gt=Gt, o=o: e.tensor_scalar(
                        out=Gt[:], in0=MOD[:, o:o + 8, :], scalar1=1.0, scalar2=None, op0=ALU.add),
                        ["MOD"], [nm])
                    P.op("dve", lambda e, Gt=Gt, GN=GN: e.tensor_tensor(
                        out=Gt[:], in0=Gt[:], in1=GN[:, :].unsqueeze(2).to_broadcast([128, 8, 2]), op=ALU.mult),
                        [nm, "GN1", "GN2"], [nm])
            P.barrier()

            def normalize(st_bufs, ci, Gt, Gname, sh_off, HTt, kb=""):
                XC, SQ, RSTD = st_bufs
                kXC, kSQ, kRS = "XC" + kb, "SQ" + kb, "RSTD" + kb
                TMP = XC
                TMP2 = XC
                c0, N = CH[ci]
                s = 1 if ci == 4 else 0
                P.dma("sp", kXC, XC[:, :, 0:N], XD[:, :, c0:c0 + N], reads=[("XD", ci)], writes=[kXC])
                P.op("act", lambda e: e.activation(out=SQ[:, :, 0:N], in_=XC[:, :, 0:N], func=AF.Square),
                     [kXC], [kSQ])
                for j in range(8):
                    P.op("pe", lambda e, j=j: e.matmul(S1[0][:, 0:N], onesB[:], SQ[:, j, 0:N],
                                                      start=(j == 0), stop=(j == 7)),
                         [kSQ, "onesB"], [S1[1]])
                P.op("act", lambda e: e.activation(out=RSTD[:, 0:N], in_=S1[0][:, 0:N], func=AF.Sqrt,
                                                   scale=1.0 / D, bias=EPS),
                     [S1[1]], [kRS])
                P.op("dve", lambda e: e.reciprocal(out=RSTD[:, 0:N], in_=RSTD[:, 0:N]),
                     [kRS], [kRS])
                P.op("dve", lambda e: e.tensor_tensor(
                    out=TMP[:, :, 0:N], in0=XC[:, :, 0:N],
                    in1=RSTD[:, 0:N].unsqueeze(1).to_broadcast([128, 8, N]), op=ALU.mult),
                    [kXC, kRS], [kXC])
                for j in range(8):
                    P.op("act", lambda e, j=j: e.activation(
                        out=HTt[:, j, c0:c0 + N], in_=TMP[:, j, 0:N], func=AF.Identity,
                        scale=Gt[:, j, s:s + 1], bias=MOD[:, sh_off + j, s:s + 1]),
                        [kXC, Gname, "MOD"], [("HT", ci)])

            with ExitStack() as slay:
                HT = sb(slay, "HT", [128, 8, NT], BF16)
                sy = ExitStack()
                YATT = sb(sy, "YATT", [128, 4, NT], BF16)
                YCV = sb(sy, "YCV", [128, 2, NT], BF16)
                YFR = sb(sy, "YFR", [128, 2, NT], BF16)
                with ExitStack() as sq_:
                    QT = sb(sq_, "QT", [128, 4, NT], BF16)
                    U = sb(sq_, "U", [128, 2, 2308], BF16)
                    ZB = sb(sq_, "ZB", [128, 2, NT], BF16)
                    P.op("pool", lambda e: e.memset(U[:], 0.0), [], ["U"])
                    with ExitStack() as st:
                        WB = sb(st, "WB", [128, 8, 1792], BF16)
                        XC = sb(st, "XC", [128, 8, 512])
                        SQ = sb(st, "SQ", [128, 8, 512], BF16)
                        RSTD = sb(st, "RSTD", [128, 512])
                        CX = sb(st, "CX", [128, 512])
                        SQ2 = sb(st, "SQ2", [128, 2, 640])
                        SS = sb(st, "SS", [128, 2, 10])
                        RQ = sb(st, "RQ", [128, 2, 10])
                        QN = sb(st, "QN", [128, 2, 640])
                        QG = sb(st, "QG", [128, 2, 640])
                        T1 = sb(st, "T1", [128, 2, 320])
                        T2 = sb(st, "T2", [128, 2, 320])
                        T3 = sb(st, "T3", [128, 2, 320])
                        T4 = sb(st, "T4", [128, 2, 320])
                        QR = sb(st, "QR", [128, 2, 640], BF16)
                        XST = sb(st, "XST", [128, 2, 512], BF16)
                        wsrc = win_in[l]
                        def wload(dst0, n, src0, key):
                            P.dma("pool", "WB", WB[:, :, dst0:dst0 + n],
                                  wsrc[:, src0:src0 + n].rearrange("(k p) n -> p k n", p=128), writes=[key])
                        wload(0, 768, OFF_B, "WB")
                        wload(768, 128, OFF_V, "WB")
                        wload(896, 256, OFF_F, "WB")
                        wload(1152, 128, OFF_K, "WB")
                        for g in range(2):
                            for jj in range(4):
                                wload(1280 + jj * 128 + g * 64, 64, OFF_Q + (g * 4 + jj) * 64, "WB")
                        nbufs = (XC, SQ, RSTD)
                        for ci, (c0, N) in enumerate(CH):
                            normalize(nbufs, ci, G1, "G1", 0, HT)
                            for t in range(2):
                                for (bk, off) in ((A0, OFF_B), (A1, OFF_C), (S1, OFF_X)):
                                    for kc in range(8):
                                        P.op("pe", lambda e, bk=bk, off=off, t=t, kc=kc, c0=c0, N=N: e.matmul(
                                            bk[0][:, 0:N], WB[:, kc, off + t * 128: off + (t + 1) * 128],
                                            HT[:, kc, c0:c0 + N], start=(kc == 0), stop=(kc == 7)),
                                            ["WB", ("HT", ci)], [bk[1]])
                                P.op("act", lambda e, N=N: e.activation(out=CX[:, 0:N], in_=S1[0][:, 0:N], func=AF.Copy),
                                     [S1[1]], ["CX"])
                                ub = (1 + c0) if ci < 4 else 2051
                                P.op("dve", lambda e, t=t, ub=ub, N=N: e.tensor_tensor(
                                    out=U[:, t, ub:ub + N], in0=A1[0][:, 0:N], in1=CX[:, 0:N], op=ALU.mult),
                                    [A1[1], "CX"], [("U", ci)])
                                P.op("act", lambda e, t=t, c0=c0, N=N: e.activation(
                                    out=ZB[:, t, c0:c0 + N], in_=A0[0][:, 0:N], func=AF.Copy),
                                    [A0[1]], [("ZB", ci, t)])
                            def tile_gen(ci, c0, ti, si):
                                PP = PC if si == 0 else PB
                                pn = "PC" if si == 0 else "PB"
                                PCk = [(pn, 0), (pn, 1)]
                                K_ = lambda nm: (nm, si)
                                SQ2s, SSs, RQs, QNs, QGs, QRs = SQ2[:, si], SS[:, si], RQ[:, si], QN[:, si], QG[:, si], QR[:, si]
                                T1s, T2s, T3s, T4s = T1[:, si], T2[:, si], T3[:, si], T4[:, si]
                                tg = c0 // 128 + ti
                                t0 = c0 + ti * 128
                                for half in range(2):
                                    for kc in range(8):
                                        P.op("pe", lambda e, half=half, kc=kc: e.matmul(
                                            PP[:, half * 512:(half + 1) * 512], HT[:, kc, t0:t0 + 128],
                                            WB[:, kc, 768 + half * 512: 768 + (half + 1) * 512],
                                            start=(kc == 0), stop=(kc == 7)),
                                            ["WB", ("HT", ci)], [(pn, half)])
                                yield
                                P.op("act", lambda e: e.activation(out=SQ2s, in_=PP[:, 384:1024], func=AF.Square),
                                     PCk, [K_("SQ2")])
                                if ci < 4:
                                    xb = tg % 2
                                    dst = XST[:, xb, :]
                                    dkey = ("XST", xb)
                                else:
                                    dst = CKV[:, ti, :]
                                    dkey = ("CKV", ti)
                                P.op("act", lambda e: e.activation(out=dst[:, 0:384], in_=PP[:, 0:384], func=AF.Copy),
                                     [(pn, 0)], [dkey])
                                yield
                                P.op("dve", lambda e: e.tensor_reduce(
                                    out=SSs, in_=SQ2s.rearrange("p (h d) -> p h d", d=64), axis=AX.X, op=ALU.add),
                                    [K_("SQ2")], [K_("SS")])
                                yield
                                P.op("act", lambda e: e.activation(out=RQs, in_=SSs, func=AF.Sqrt, scale=1.0 / 64, bias=EPS),
                                     [K_("SS")], [K_("RQ")])
                                yield
                                P.op("dve", lambda e: e.reciprocal(out=RQs, in_=RQs), [K_("RQ")], [K_("RQ")])
                                yield
                                P.op("dve", lambda e: e.tensor_tensor(
                                    out=QNs.rearrange("p (h d) -> p h d", d=64),
                                    in0=PP[:, 384:1024].rearrange("p (h d) -> p h d", d=64),
                                    in1=RQs.unsqueeze(2).to_broadcast([128, 10, 64]), op=ALU.mult),
                                    PCk + [K_("RQ")], [K_("QN")])
                                yield
                                P.op("pool", lambda e: e.tensor_tensor(out=QGs, in0=QNs, in1=GQK[:], op=ALU.mult),
                                     [K_("QN"), "GQK"], [K_("QG")])
                                yield
                                QG3 = QGs.rearrange("p (h d) -> p h d", d=64)
                                QR3 = QRs.rearrange("p (h d) -> p h d", d=64)
                                if ci < 4:
                                    cb = RC[:, tg, :].unsqueeze(1).to_broadcast([128, 10, 32])
                                    sbb = RS[:, tg, :].unsqueeze(1).to_broadcast([128, 10, 32])
                                    v3 = lambda T: T.rearrange("p (h d) -> p h d", d=32)
                                    P.op("dve", lambda e: e.tensor_tensor(out=v3(T1s), in0=QG3[:, :, 0:32], in1=cb, op=ALU.mult),
                                         [K_("QG"), "RC"], [K_("T1")])
                                    P.op("pool", lambda e: e.tensor_tensor(out=v3(T2s), in0=QG3[:, :, 32:64], in1=sbb, op=ALU.mult),
                                         [K_("QG"), "RS"], [K_("T2")])
                                    yield
                                    P.op("pool", lambda e: e.tensor_tensor(out=v3(T3s), in0=QG3[:, :, 0:32], in1=sbb, op=ALU.mult),
                                         [K_("QG"), "RS"], [K_("T3")])
                                    P.op("dve", lambda e: e.tensor_tensor(out=v3(T4s), in0=QG3[:, :, 32:64], in1=cb, op=ALU.mult),
                                         [K_("QG"), "RC"], [K_("T4")])
                                    yield
                                    P.op("dve", lambda e: e.tensor_tensor(out=QR3[:, :, 0:32], in0=v3(T1s), in1=v3(T2s), op=ALU.subtract),
                                         [K_("T1"), K_("T2")], [K_("QRa")])
                                    P.op("pool", lambda e: e.tensor_tensor(out=QR3[:, :, 32:64], in0=v3(T3s), in1=v3(T4s), op=ALU.add),
                                         [K_("T3"), K_("T4")], [K_("QRb")])
                                else:
                                    P.op("pool", lambda e: e.tensor_copy(out=QRs, in_=QGs), [K_("QG")], [K_("QRa"), K_("QRb")])
                                yield
                                P.op("pool", lambda e: e.tensor_copy(out=dst[:, 384:512], in_=QRs[:, 0:128]),
                                     [K_("QRa"), K_("QRb")], [(dkey, "k")])
                                if ci < 4:
                                    P.dma("sp", "XST%d" % xb, XIN[l].ap()[t0:t0 + 128, :], dst,
                                          reads=[dkey, (dkey, "k")], writes=["XIN"])
                                for i in range(4):
                                    P.op("pe", lambda e, i=i: e.transpose(
                                        PT[:, i * 128:(i + 1) * 128], QRs[:, 128 + i * 128: 256 + i * 128], identB[:]),
                                        [K_("QRa"), K_("QRb"), "identB"], ["PT"])
                                P.op("dve", lambda e: e.tensor_copy(
                                    out=QT[:, :, t0:t0 + 128], in_=PT[:, 0:512].rearrange("p (j t) -> p j t", t=128)),
                                    ["PT"], [("QT", tg)])
                                yield

                            for tp in range(0, N // 128, 2):
                                gens = [tile_gen(ci, c0, tp + si, si) for si in range(2)]
                                alive = True
                                while alive:
                                    alive = False
                                    for gx in gens:
                                        try:
                                            next(gx)
                                            alive = True
                                        except StopIteration:
                                            pass
                        for t in range(2):
                            P.op("dve", lambda e, t=t: e.tensor_copy(out=HS[:, 2 * t:2 * t + 1], in_=U[:, t, 1:2]), [("U", 0)], ["HS"])
                            P.op("dve", lambda e, t=t: e.tensor_copy(out=HS[:, 2 * t + 1:2 * t + 2], in_=U[:, t, 2048:2049]), [("U", 3)], ["HS"])
                        P.dma("sp", "HS", HIN[l].ap(), HS[:], reads=["HS"], writes=["HIN"])
                    P.barrier()
                    if debug and l == 0:
                        P.dma("sp", "dbg", dd["d_h"], HT[:], writes=["dbg"])
                        P.dma("sp", "dbg", dd["d_qt"], QT[:], writes=["dbg"])
                        P.dma("sp", "dbg", dd["d_mod"], MOD[:], writes=["dbg"])
                    P.custom("pool", "cc", 1, lambda e, l=l: e.collective_compute(
                        "AllGather", ALU.bypass, replica_groups=RG,
                        ins=[XIN[l].ap().opt()], outs=[XOUT[l].ap().opt()]),
                        reads=["XIN"], writes=["XOUT"])
                    P.custom("pool", "cc", 1, lambda e, l=l: e.collective_compute(
                        "AllGather", ALU.bypass, replica_groups=RG,
                        ins=[HIN[l].ap().opt()], outs=[HOUT[l].ap().opt()]),
                        reads=["HIN"], writes=["HOUT"])
                    P.barrier()
                    with ExitStack() as st:
                        ACC = sb(st, "ACC", [128, 2, 512])
                        P.dma("sp", "HG", HG[:], HOUT[l].ap().rearrange("(r p) f -> p r f", p=128),
                              reads=["HOUT"], writes=["HG"])
                        for t in range(2):
                            P.op("dve", lambda e, t=t: e.tensor_tensor(out=U[:, t, 0:1], in0=HG[:, 0, 2 * t + 1:2 * t + 2],
                                                                      in1=SEL[:, 0:1], op=ALU.mult), ["HG", "SEL", "U"], ["U"])
                            P.op("dve", lambda e, t=t: e.tensor_tensor(out=U[:, t, 2049:2050], in0=HG[:, 1, 2 * t:2 * t + 1],
                                                                      in1=SEL[:, 1:2], op=ALU.mult), ["HG", "SEL", "U"], ["U"])
                        for ci, (c0, N) in enumerate(CH):
                            b = c0 if ci < 4 else 2050
                            for t in range(2):
                                ak = ("ACC", t)
                                P.op("dve", lambda e, t=t, b=b, N=N: e.tensor_scalar(
                                    out=ACC[:, t, 0:N], in0=U[:, t, b:b + N], scalar1=WCV[:, t, 0:1], scalar2=None, op0=ALU.mult),
                                    ["U", "WCV"], [ak])
                                for jj in (1, 2):
                                    P.op("dve", lambda e, t=t, b=b, N=N, jj=jj: e.scalar_tensor_tensor(
                                        out=ACC[:, t, 0:N], in0=U[:, t, b + jj:b + jj + N], scalar=WCV[:, t, jj:jj + 1],
                                        in1=ACC[:, t, 0:N], op0=ALU.mult, op1=ALU.add),
                                        ["U", "WCV", ak], [ak])
                                P.op("pool", lambda e, t=t, c0=c0, N=N: e.tensor_tensor(
                                    out=YCV[:, t, c0:c0 + N], in0=ACC[:, t, 0:N], in1=ZB[:, t, c0:c0 + N], op=ALU.mult),
                                    [ak, "ZB"], [("YCV", ci)])
                    P.barrier()
                    with ExitStack() as st:
                        KST = sb(st, "KST", [128, 34, 128], BF16)
                        KT = sb(st, "KT", [128, 34 * 128], BF16)
                        VA = sb(st, "VA", [128, 34, 2, 65], BF16)
                        E = sb(st, "E", [128, 3, 1024], BF16)
                        OA = sb(st, "OA", [128, 2, 512])
                        RR = sb(st, "RR", [128, 2, 512])
                        YT = sb(st, "YT", [128, 2, 512], BF16)
                        xo = XOUT[l].ap()
                        P.op("pool", lambda e: e.memset(VA[:], 1.0), [], ["VA"])
                        P.dma("sp", "KST", KST[:, 0:32, :], xo[:, 384:512].rearrange("(t p) c -> p t c", p=128),
                              reads=["XOUT"], writes=["KST"])
                        for hh in range(2):
                            for g in range(2):
                                P.dma("sp", "VA", VA[:, hh * 16:(hh + 1) * 16, g, 0:64],
                                      xo[hh * 2048:(hh + 1) * 2048, g * 64:(g + 1) * 64].rearrange("(t p) d -> p t d", p=128),
                                      reads=["XOUT", "VA"] if (hh == 0 and g == 0) else ["XOUT"], writes=["VA"])
                        P.op("pool", lambda e: e.tensor_copy(out=KST[:, 32:34, :], in_=CKV[:, :, 384:512]), [], ["KST"])
                        P.op("pool", lambda e: e.tensor_copy(
                            out=VA[:, 32:34, :, 0:64], in_=CKV[:, :, 0:128].rearrange("p t (g d) -> p t g d", d=64)), [], ["VA"])
                        for g0 in range(0, 34, 8):
                            n = min(8, 34 - g0)
                            for i in range(n):
                                P.op("pe", lambda e, i=i, g0=g0: e.transpose(
                                    PT[:, i * 128:(i + 1) * 128], KST[:, g0 + i, :], identB[:]), ["KST"], ["PT"])
                            P.op("dve", lambda e, g0=g0, n=n: e.tensor_copy(out=KT[:, g0 * 128:(g0 + n) * 128], in_=PT[:, 0:n * 128]),
                                 ["PT"], ["KT"])
                        spair = [(PA, [("PA", 0), ("PA", 1)]), (PB, [("PB", 0), ("PB", 1)])]
                        its = []
                        for ci, (c0, N) in enumerate(CH):
                            keys = list(range(34)) if ci < 4 else [32, 33]
                            for j in range(4):
                                for ki, kt in enumerate(keys):
                                    its.append(dict(ci=ci, c0=c0, N=N, j=j, ki=ki, kt=kt, nk=len(keys), n=len(its)))

                        def a_scores(itd):
                            sp_t, sp_k = spair[itd["n"] % 2]
                            kt, j, c0, N = itd["kt"], itd["j"], itd["c0"], itd["N"]
                            for g in range(2):
                                P.op("pe", lambda e, sp_t=sp_t, g=g, kt=kt, j=j, c0=c0, N=N: e.matmul(
                                    sp_t[:, g * 512:g * 512 + N], KT[g * 64:(g + 1) * 64, kt * 128:(kt + 1) * 128],
                                    QT[g * 64:(g + 1) * 64, j, c0:c0 + N], start=True, stop=True),
                                    ["KT", "QT"], [sp_k[g]])

                        def a_exp(itd):
                            sp_t, sp_k = spair[itd["n"] % 2]
                            eb = itd["n"] % 3
                            N = itd["N"]
                            P.op("act", lambda e, sp_t=sp_t, eb=eb, N=N: e.activation(
                                out=E[:, eb, :].rearrange("p (g n) -> p g n", g=2)[:, :, 0:N],
                                in_=sp_t[:, :].rearrange("p (g n) -> p g n", g=2)[:, :, 0:N],
                                func=AF.Exp, scale=0.125), sp_k, [("E", eb)])

                        def a_pv(itd):
                            eb = itd["n"] % 3
                            kt, j, c0, N, ki, nk, ci = itd["kt"], itd["j"], itd["c0"], itd["N"], itd["ki"], itd["nk"], itd["ci"]
                            for g in range(2):
                                ob = C0 if g == 0 else C1
                                P.op("pe", lambda e, ob=ob, g=g, kt=kt, eb=eb, N=N, ki=ki, nk=nk: e.matmul(
                                    ob[0][0:65, 0:N], VA[:, kt, g, :], E[:, eb, g * 512:g * 512 + N],
                                    start=(ki == 0), stop=(ki == nk - 1)),
                                    ["VA", ("E", eb)], [ob[1]])
                            if ki != nk - 1:
                                return
                            for g in range(2):
                                ob = C0 if g == 0 else C1
                                P.op("dve", lambda e, ob=ob, g=g, N=N: e.tensor_copy(out=OA[0:64, g, 0:N], in_=ob[0][0:64, 0:N]),
                                     [ob[1]], [("OA", g)])
                                P.op("dve", lambda e, ob=ob, g=g, N=N: e.reciprocal(out=RR[64:65, g, 0:N], in_=ob[0][64:65, 0:N]),
                                     [ob[1]], [("RR", g)])
                                P.op("pe", lambda e, g=g, N=N: e.matmul(S1[0][0:64, 0:N], onesF[64:65, 0:64], RR[64:65, g, 0:N],
                                                                     start=True, stop=True),
                                     [("RR", g), "onesF"], [S1[1]])
                                if g == 0:
                                    P.op("dve", lambda e, g=g, j=j, c0=c0, N=N: e.tensor_tensor(
                                        out=YATT[0:64, j, c0:c0 + N], in0=OA[0:64, g, 0:N], in1=S1[0][0:64, 0:N], op=ALU.mult),
                                        [("OA", g), S1[1]], [("YATT", ci, j, 0)])
                                else:
                                    yb = (ci * 4 + j) % 2
                                    P.op("dve", lambda e, g=g, yb=yb, N=N: e.tensor_tensor(
                                        out=YT[0:64, yb, 0:N], in0=OA[0:64, g, 0:N], in1=S1[0][0:64, 0:N], op=ALU.mult),
                                        [("OA", g), S1[1]], [("YT", yb)])
                                    P.dma("sp", "YT%d" % yb, YATT[64:128, j, c0:c0 + N], YT[0:64, yb, 0:N],
                                          reads=[("YT", yb)], writes=[("YATT", ci, j, 1)])

                        a_scores(its[0])
                        for n, itd in enumerate(its):
                            if n + 1 < len(its):
                                a_scores(its[n + 1])
                            a_exp(itd)
                            a_pv(itd)
                    P.barrier()
                with ExitStack() as st:
                    DTB = sb(st, "DTB", [128, 2, 2, 4, 512], BF16)
                    ZA = sb(st, "ZA", [128, 16, 256], BF16)
                    ZR = sb(st, "ZR", [128, 16, 256], BF16)
                    ZE = sb(st, "ZE", [128, 16, 256], BF16)
                    ZO = sb(st, "ZO", [128, 16, 256], BF16)
                    AB = sb(st, "AB", [128, 4, 512], BF16)
                    xo = XOUT[l].ap()
                    P.dma("sp", "ZA", ZA[:], xo[0:2048, 128:384].rearrange("(t p) c -> p t c", p=128), writes=["ZA"])
                    P.dma("sp", "ZR", ZR[:], xo[2048:4096, 128:384].rearrange("(t p) c -> p t c", p=128), writes=["ZR"])
                    P.op("dve", lambda e: e.tensor_tensor(out=ZE[:], in0=ZA[:], in1=ZR[:], op=ALU.add), ["ZA", "ZR"], ["ZE"])
                    P.op("pool", lambda e: e.tensor_tensor(out=ZO[:], in0=ZA[:], in1=ZR[:], op=ALU.subtract), ["ZA", "ZR"], ["ZO"])
                    accs = [A0, A1, B0, B1]
                    it = 0
                    for ci, (c0, N) in enumerate(CH):
                        if ci < 4:
                            for par in range(2):
                                Zs = ZE if par == 0 else ZO
                                cs = slice(c0 + par * 256, c0 + (par + 1) * 256)
                                for tg in range(4):
                                    bf = it % 2
                                    it += 1
                                    rows = slice(tg * 512, (tg + 1) * 512)
                                    P.dma("sp", "DTB%d" % bf, DTB[:, bf, 0, :, 0:256], dcos_in[rows, cs].rearrange("(t p) k -> p t k", p=128),
                                          writes=[("DTB", bf)])
                                    P.dma("sp", "DTB%d" % bf, DTB[:, bf, 1, :, 0:256], dsin_in[rows, cs].rearrange("(t p) k -> p t k", p=128),
                                          writes=[("DTB", bf)])
                                    for t in range(4):
                                        tt = tg * 4 + t
                                        for c2 in range(2):
                                            for ri in range(2):
                                                ak = accs[c2 * 2 + ri]
                                                P.op("pe", lambda e, ak=ak, Zs=Zs, bf=bf, t=t, tt=tt, c2=c2, ri=ri, par=par: e.matmul(
                                                    ak[0][:, par * 256:(par + 1) * 256], Zs[:, tt, c2 * 128:(c2 + 1) * 128],
                                                    DTB[:, bf, ri, t, 0:256],
                                                    start=(tt == 0), stop=(tt == 15)),
                                                    ["ZE", "ZO", ("DTB", bf)], [ak[1]])
                        else:
                            for c2 in range(2):
                                for ri in range(2):
                                    ak = accs[c2 * 2 + ri]
                                    tb = DCC if ri == 0 else DCS
                                    for t in range(2):
                                        P.op("pe", lambda e, ak=ak, tb=tb, t=t, c2=c2: e.matmul(
                                            ak[0][:, 0:256], CKV[:, t, 128 + c2 * 128:256 + c2 * 128], tb[:, t, :],
                                            start=(t == 0), stop=(t == 1)), ["DCC", "DCS"], [ak[1]])
                        for q in range(4):
                            ak = accs[q]
                            if q % 2 == 0:
                                P.op("act", lambda e, ak=ak, q=q, N=N: e.activation(out=AB[:, q, 0:N], in_=ak[0][:, 0:N], func=AF.Copy),
                                     [ak[1]], [("AB", q)])
                            else:
                                P.op("dve", lambda e, ak=ak, q=q, N=N: e.tensor_copy(out=AB[:, q, 0:N], in_=ak[0][:, 0:N]),
                                     [ak[1]], [("AB", q)])
                        for c2 in range(2):
                            ob = C0 if c2 == 0 else C1
                            P.op("pe", lambda e, ob=ob, c2=c2, N=N: e.matmul(ob[0][:, 0:N], CCB[:], AB[:, c2 * 2, 0:N], start=True, stop=False),
                                 [("AB", c2 * 2), "CCB"], [ob[1]])
                            P.op("pe", lambda e, ob=ob, c2=c2, N=N: e.matmul(ob[0][:, 0:N], SCB[:], AB[:, c2 * 2 + 1, 0:N], start=False, stop=True),
                                 [("AB", c2 * 2 + 1), "SCB"], [ob[1]])
                            if ci < 4:
                                P.op("dve", lambda e, ob=ob, c2=c2, c0=c0, N=N: e.tensor_copy(
                                    out=YFR[:, c2, c0:c0 + N].rearrange("p (i two) -> p two i", two=2),
                                    in_=ob[0][:, 0:N].rearrange("p (two i) -> p two i", two=2)),
                                    [ob[1]], [("YFR", ci)])
                            else:
                                P.op("dve", lambda e, ob=ob, c2=c2, c0=c0, N=N: e.tensor_copy(out=YFR[:, c2, c0:c0 + N], in_=ob[0][:, 0:N]),
                                     [ob[1]], [("YFR", ci)])
                P.barrier()
                nch = 4 if last else 5
                if debug and l == 0:
                    P.dma("sp", "dbg", dd["d_ycv"], YCV[:], writes=["dbg"])
                    P.dma("sp", "dbg", dd["d_yfr"], YFR[:], writes=["dbg"])
                    P.dma("sp", "dbg", dd["d_yatt"], YATT[:], writes=["dbg"])
                    P.dma("sp", "dbg", dd["d_xout"], XOUT[0].ap(), writes=["dbg"])
                    P.dma("sp", "dbg", dd["d_hg"], HG[:], writes=["dbg"])
                    P.barrier()
                with ExitStack() as st:
                    WG = sb(st, "WG", [128, 8, 3072], BF16)
                    WCO = sb(st, "WCO", [128, 2, D], BF16)
                    WFO = sb(st, "WFO", [128, 2, D], BF16)
                    WAO = sb(st, "WAO", [128, 4, D], BF16)
                    WO = sb(st, "WO", [128, 8, D], BF16)
                    GS = sb(st, "GS", [128, 3, 512], BF16)
                    M1 = sb(st, "M1", [128, 3, 512])
                    MT = sb(st, "MT", [128, 8, 512], BF16)
                    XP = sb(st, "XP", [128, 2, 512])
                    for b in range(3):
                        P.dma("pool", "WG", WG[:, :, b * 1024:(b + 1) * 1024],
                              win_in[l][:, OFF_G + b * 1024: OFF_G + (b + 1) * 1024].rearrange("(k p) n -> p k n", p=128),
                              writes=["WG"])
                    P.dma("pool", "WCO", WCO[:], wco_in[l].rearrange("(k p) n -> p k n", p=128), writes=["WCO"])
                    P.dma("pool", "WFO", WFO[:], wfo_in[l].rearrange("(k p) n -> p k n", p=128), writes=["WFO"])
                    for g in range(2):
                        P.dma("pool", "WAO", WAO[g * 64:(g + 1) * 64, :, :],
                              wao_in[l][g * 256:(g + 1) * 256, :].rearrange("(j d) n -> d j n", d=64), writes=["WAO"])
                    P.dma("pool", "WO", WO[:], wo_in[l].rearrange("(k p) n -> p k n", p=128), writes=["WO"])
                    ycs = lambda ci: [("YCV", ci)]
                    for ci in range(nch):
                        c0, N = CH[ci]
                        s = 1 if ci == 4 else 0
                        for m in range(8):
                            ms = slice(m * 128, (m + 1) * 128)
                            gb = [B1, C0, C1]
                            for b in range(3):
                                for kc in range(8):
                                    P.op("pe", lambda e, b=b, kc=kc, m=m, c0=c0, N=N: e.matmul(
                                        gb[b][0][:, 0:N], WG[:, kc, b * 1024 + m * 128: b * 1024 + (m + 1) * 128],
                                        HT[:, kc, c0:c0 + N], start=(kc == 0), stop=(kc == 7)), ["WG"], [gb[b][1]])
                                P.op("act", lambda e, b=b, N=N: e.activation(out=GS[:, b, 0:N], in_=gb[b][0][:, 0:N], func=AF.Sigmoid),
                                     [gb[b][1]], [("GS", b)])
                            for t in range(2):
                                P.op("pe", lambda e, t=t, ms=ms, c0=c0, N=N: e.matmul(A0[0][:, 0:N], WCO[:, t, ms], YCV[:, t, c0:c0 + N],
                                                                                   start=(t == 0), stop=(t == 1)), ["WCO"], [A0[1]])
                            for t in range(2):
                                P.op("pe", lambda e, t=t, ms=ms, c0=c0, N=N: e.matmul(A1[0][:, 0:N], WFO[:, t, ms], YFR[:, t, c0:c0 + N],
                                                                                   start=(t == 0), stop=(t == 1)), ["WFO"], [A1[1]])
                            for j in range(4):
                                P.op("pe", lambda e, j=j, ms=ms, c0=c0, N=N: e.matmul(B0[0][:, 0:N], WAO[:, j, ms], YATT[:, j, c0:c0 + N],
                                                                                   start=(j == 0), stop=(j == 3)), ["WAO"], [B0[1]])
                            pb = [A0, A1, B0]
                            for b in range(3):
                                P.op("dve", lambda e, b=b, N=N: e.tensor_tensor(out=M1[:, b, 0:N], in0=pb[b][0][:, 0:N], in1=GS[:, b, 0:N], op=ALU.mult),
                                     [pb[b][1], ("GS", b)], [("M1", b)])
                            P.op("pool", lambda e, N=N: e.tensor_tensor(out=M1[:, 0, 0:N], in0=M1[:, 0, 0:N], in1=M1[:, 1, 0:N], op=ALU.add),
                                 [("M1", 0), ("M1", 1)], [("M1", 0)])
                            P.op("pool", lambda e, m=m, N=N: e.tensor_tensor(out=MT[:, m, 0:N], in0=M1[:, 0, 0:N], in1=M1[:, 2, 0:N], op=ALU.add),
                                 [("M1", 0), ("M1", 2)], [("MT", m)])
                        for m2 in range(8):
                            xb = m2 % 2
                            P.dma("sp", "XP%d" % xb, XP[:, xb, 0:N], XD[:, m2, c0:c0 + N], reads=[("XD", ci, m2)], writes=[("XP", xb)])
                            rk = S1 if m2 % 2 == 0 else B1
                            for m in range(8):
                                P.op("pe", lambda e, rk=rk, m=m, m2=m2, N=N: e.matmul(rk[0][:, 0:N], WO[:, m, m2 * 128:(m2 + 1) * 128], MT[:, m, 0:N],
                                                                           start=(m == 0), stop=(m == 7)), ["WO", ("MT", m)], [rk[1]])
                            P.op("dve", lambda e, rk=rk, xb=xb, m2=m2, s=s, N=N: e.scalar_tensor_tensor(
                                out=XP[:, xb, 0:N], in0=rk[0][:, 0:N], scalar=MOD[:, 16 + m2, s:s + 1], in1=XP[:, xb, 0:N],
                                op0=ALU.mult, op1=ALU.add), [rk[1], ("XP", xb)], [("XP", xb)])
                            P.dma("sp", "XP%d" % xb, XD[:, m2, c0:c0 + N], XP[:, xb, 0:N], reads=[("XP", xb)], writes=[("XD", ci, m2)])
                P.barrier()
                sy.close()
                if debug and l == 0:
                    P.dma("sp", "dbg", dd["d_xmid"], XD, writes=["dbg"])
                    P.barrier()
                with ExitStack() as st:
                    nb2 = [(sb(st, "XC2", [128, 8, 512]), sb(st, "SQb", [128, 8, 512], BF16), sb(st, "RSTD2", [128, 512]))
                           for _ in range(2)]
                    for ci in range(nch):
                        normalize(nb2[ci % 2], ci, G2, "G2", 24, HT, kb="h%d" % (ci % 2))
                P.barrier()
                with ExitStack() as st:
                    W1 = sb(st, "W1", [128, 2, 8, 2048], BF16)
                    W2 = sb(st, "W2", [128, 2, 16, D], BF16)
                    RT = sb(st, "RT", [128, 2, 512], BF16)
                    AT = sb(st, "AT", [128, 16, 512], BF16)
                    XP = sb(st, "XPb", [128, 2, 512])
                    for hf in range(2):
                        for q in range(2):
                            P.dma("pool", "W1%d" % hf, W1[:, hf, :, q * 1024:(q + 1) * 1024],
                                  w1_in[l][:, hf * 2048 + q * 1024: hf * 2048 + (q + 1) * 1024].rearrange("(k p) n -> p k n", p=128),
                                  writes=[("W1", hf)])
                            P.dma("pool", "W2%d" % hf, W2[:, hf, q * 8:(q + 1) * 8, :],
                                  w2_in[l][hf * 2048 + q * 1024: hf * 2048 + (q + 1) * 1024, :].rearrange("(k p) n -> p k n", p=128),
                                  writes=[("W2", hf)])
                    for hf in range(2):
                        for ci in range(nch):
                            c0, N = CH[ci]
                            s = 1 if ci == 4 else 0
                            ub = [A0, A1, B0, B1]
                            for f in range(16):
                                bk = ub[f % 4]
                                rb = f % 2
                                for kc in range(8):
                                    P.op("pe", lambda e, bk=bk, f=f, kc=kc, c0=c0, N=N, hf=hf: e.matmul(
                                        bk[0][:, 0:N], W1[:, hf, kc, f * 128:(f + 1) * 128], HT[:, kc, c0:c0 + N],
                                        start=(kc == 0), stop=(kc == 7)), [("W1", hf), ("HT", ci)], [bk[1]])
                                P.op("act", lambda e, bk=bk, rb=rb, N=N: e.activation(out=RT[:, rb, 0:N], in_=bk[0][:, 0:N], func=AF.Relu),
                                     [bk[1]], [("RT", rb)])
                                sq_eng = "pool" if f % 2 == 0 else "dve"
                                P.op(sq_eng, lambda e, rb=rb, f=f, N=N: e.tensor_tensor(out=AT[:, f, 0:N], in0=RT[:, rb, 0:N], in1=RT[:, rb, 0:N], op=ALU.mult),
                                     [("RT", rb)], [("AT", f)])
                            for m2 in range(8):
                                xb = m2 % 2
                                rk = C0 if m2 % 2 == 0 else C1
                                P.dma("sp", "XQ%d" % xb, XP[:, xb, 0:N], XD[:, m2, c0:c0 + N], reads=[("XD", ci, m2)], writes=[("XPb", xb)])
                                for f in range(16):
                                    P.op("pe", lambda e, rk=rk, f=f, m2=m2, N=N, hf=hf: e.matmul(rk[0][:, 0:N], W2[:, hf, f, m2 * 128:(m2 + 1) * 128], AT[:, f, 0:N],
                                                                                      start=(f == 0), stop=(f == 15)), [("W2", hf), ("AT", f)], [rk[1]])
                                P.op("dve", lambda e, rk=rk, xb=xb, m2=m2, s=s, N=N: e.scalar_tensor_tensor(
                                    out=XP[:, xb, 0:N], in0=rk[0][:, 0:N], scalar=MOD[:, 40 + m2, s:s + 1], in1=XP[:, xb, 0:N],
                                    op0=ALU.mult, op1=ALU.add), [rk[1], ("XPb", xb)], [("XPb", xb)])
                                P.dma("sp", "XQ%d" % xb, XD[:, m2, c0:c0 + N], XP[:, xb, 0:N], reads=[("XPb", xb)], writes=[("XD", ci, m2)])
                P.barrier()
            if debug:
                P.dma("sp", "dbg", dbg_out[l], XD, writes=["dbg"])
                P.barrier()

        with ExitStack() as st:
            XC = sb(st, "XCf", [128, 8, 512])
            YO = sb(st, "YO", [128, 2, D])
            it = 0
            for ci in range(4):
                c0, N = CH[ci]
                P.dma("sp", "XCf", XC[:], XD[:, :, c0:c0 + N], writes=["XCf"])
                for ti in range(4):
                    yb = it % 2
                    pp, pk = [(PA, [("PA", 0), ("PA", 1)]), (PB, [("PB", 0), ("PB", 1)])][it % 2]
                    it += 1
                    for j in range(8):
                        P.op("pe", lambda e, pp=pp, j=j, ti=ti: e.transpose(pp[:, j * 128:(j + 1) * 128], XC[:, j, ti * 128:(ti + 1) * 128], identF[:]),
                             ["XCf"], [pk[j // 4]])
                    P.op("dve", lambda e, pp=pp, yb=yb: e.tensor_copy(out=YO[:, yb, 0:512], in_=pp[:, 0:512]), [pk[0]], [("YO", yb, 0)])
                    P.op("act", lambda e, pp=pp, yb=yb: e.activation(out=YO[:, yb, 512:1024], in_=pp[:, 512:1024], func=AF.Copy), [pk[1]], [("YO", yb, 1)])
                    P.dma("sp", "YO%d" % yb, y_out[c0 + ti * 128: c0 + (ti + 1) * 128, :], YO[:, yb, :],
                          reads=[("YO", yb, 0), ("YO", yb, 1)], writes=["y"])
        P.barrier()
        P.emit()
    return nc


def _consts():
    bf = ml_dtypes.bfloat16
    n = np.arange(4096, dtype=np.int64)
    out = {}
    for h in range(2):
        kk = np.arange(2048, dtype=np.int64).reshape(4, 256, 2).transpose(0, 2, 1).reshape(-1)
        k = h * 2048 + kk
        ph = ((n[:2048, None] * k[None, :]) % 4096).astype(np.float64) * (2 * np.pi / 4096)
        out[("dcos", h)] = (np.cos(ph) / 64.0).astype(np.float32).astype(bf)
        out[("dsin", h)] = (np.sin(ph) / 64.0).astype(np.float32).astype(bf)
        pos = np.arange(h * 2048, (h + 1) * 2048)
        row, col = pos // 64, pos % 64
        inv = 10000.0 ** (-np.arange(16, dtype=np.float32) / 16)
        ang = np.concatenate([row[:, None].astype(np.float32) * inv, col[:, None].astype(np.float32) * inv], axis=-1)
        c = np.cos(ang).astype(np.float32).reshape(16, 128, 32).transpose(1, 0, 2)
        s = np.sin(ang).astype(np.float32).reshape(16, 128, 32).transpose(1, 0, 2)
        out[("rcos", h)] = np.ascontiguousarray(c)
        out[("rsin", h)] = np.ascontiguousarray(s)
        sel = np.zeros((128, 2), np.float32)
        sel[:, 0] = 1.0 if h == 1 else 0.0
        sel[:, 1] = 1.0 if h == 0 else 0.0
        out[("sel", h)] = sel
    m = np.arange(256, dtype=np.int64)
    ph = ((m[:, None] * m[None, :]) % 256).astype(np.float64) * (2 * np.pi / 256)
    out["dcc"] = (np.cos(ph) / 16.0).astype(np.float32).astype(bf)
    out["dcs"] = (np.sin(ph) / 16.0).astype(np.float32).astype(bf)
    c = np.arange(64, dtype=np.int64)
    ph = ((c[:, None] * c[None, :]) % 64).astype(np.float64) * (2 * np.pi / 64)
    cc = np.zeros((128, 128), np.float32)
    sc = np.zeros((128, 128), np.float32)
    for g in range(2):
        cc[g * 64:(g + 1) * 64, g * 64:(g + 1) * 64] = np.cos(ph) / 8.0
        sc[g * 64:(g + 1) * 64, g * 64:(g + 1) * 64] = -np.sin(ph) / 8.0
    out["ccb"] = cc.astype(bf)
    out["scb"] = sc.astype(bf)
    out["ident"] = np.eye(128, dtype=np.float32)
    return out


def make_in_maps(inputs):
    f = lambda a: np.ascontiguousarray(np.asarray(a, dtype=np.float32))
    x, c, ctx, c_ctx = f(inputs["x"]), f(inputs["c"]), f(inputs["ctx"]), f(inputs["c_ctx"])
    K = _consts()
    pt = lambda v, t: np.ascontiguousarray(v.reshape(v.shape[0], t, 128).transpose(0, 2, 1))
    bm = pt(f(inputs["b_mod"]), 48)
    gn1 = pt(f(inputs["g_norm1"]), 8)
    gn2 = pt(f(inputs["g_norm2"]), 8)
    wc = f(inputs["w_conv"])
    wcv = np.ascontiguousarray(wc.reshape(DEPTH, 3, 2, 128).transpose(0, 3, 2, 1))
    gq, gk = f(inputs["g_q"]), f(inputs["g_k"])
    row = np.concatenate([np.tile(gk, (1, 2)), np.tile(gq, (1, 8))], axis=1)
    gqk = np.ascontiguousarray(np.broadcast_to(row[:, None, :], (DEPTH, 128, 640)))
    shared = {
        "w_mod": f(inputs["w_mod"]), "bm": bm, "gn1": gn1, "gn2": gn2, "w_in": f(inputs["w_in"]),
        "wcv": wcv, "gqk": gqk, "w_conv_out": f(inputs["w_conv_out"]), "w_four_out": f(inputs["w_four_out"]),
        "w_attn_out": f(inputs["w_attn_out"]), "w_o": f(inputs["w_o"]), "w_ff1": f(inputs["w_ff1"]),
        "w_ff2": f(inputs["w_ff2"]), "dcc": K["dcc"], "dcs": K["dcs"], "ccb": K["ccb"], "scb": K["scb"],
        "ident": K["ident"],
    }
    maps = []
    for core in range(8):
        b, h = core // 2, core % 2
        cc = np.stack([c[b].reshape(8, 128).T, c_ctx.reshape(8, 128).T], axis=-1)
        m = dict(shared)
        m.update({
            "x": np.ascontiguousarray(x[b, h * 2048:(h + 1) * 2048, :]),
            "ctx": np.ascontiguousarray(ctx[b]),
            "cc": np.ascontiguousarray(cc.astype(np.float32)),
            "rcos": K[("rcos", h)], "rsin": K[("rsin", h)], "dcos": K[("dcos", h)], "dsin": K[("dsin", h)],
            "sel": K[("sel", h)],
        })
        maps.append(m)
    return maps


_NC = {}


def kernel(**inputs):
    if "nc" not in _NC:
        _NC["nc"] = build(debug=True)
    nc = _NC["nc"]
    maps = make_in_maps(inputs)
    res = run_bass_kernel_spmd(nc, maps, core_ids=list(range(8)))
    out = np.empty((4, 4096, D), np.float32)
    for core in range(8):
        b, h = core // 2, core % 2
        out[b, h * 2048:(h + 1) * 2048, :] = res.results[core]["y"]
    return out
```
